# Optimizing a Trainium2 kernel written in Bass

```python
import jax, jax.numpy as jnp
from jax import lax

D_MODEL = 2048
BATCH = 8
SEQ = 2048
DEPTH = 1

CHUNK = 64
N_META = 16
D_RNN = D_MODEL
LRU_BLOCKS = 16
LRU_BS = D_RNN // LRU_BLOCKS
CONV_W = 4
LRU_C = 8.0
N_HEADS = 16
HEAD_DIM = 128
D_ATTN = N_HEADS * HEAD_DIM
Q_BLOCK = 128
D_FF = 5632
N_NORMS = 6
EPS = 1e-6
SPLITS = (D_RNN, D_RNN, D_ATTN, D_ATTN, D_ATTN, N_HEADS, D_MODEL, D_MODEL)
N_IN = sum(SPLITS)

kernel_name = "hybrid_rglru_fox_macaron_block"


def rms_norm(x, g):
    xf = x.astype(jnp.float32)
    y = xf * lax.rsqrt(jnp.mean(xf * xf, axis=-1, keepdims=True) + EPS)
    return (y * g.astype(jnp.float32)).astype(x.dtype)


def swiglu(x, w_gu, w_down):
    gu = x @ w_gu
    gate, up = jnp.split(gu, 2, axis=-1)
    return (jax.nn.silu(gate) * up) @ w_down


def causal_depthwise_conv(x, w, b):
    T = x.shape[1]
    xp = jnp.pad(x, ((0, 0), (CONV_W - 1, 0), (0, 0)))
    y = xp[:, 0:T] * w[0]
    for k in range(1, CONV_W):
        y = y + xp[:, k:k + T] * w[k]
    return y + b


def _lru_combine(e1, e2):
    a1, b1 = e1
    a2, b2 = e2
    return a1 * a2, a2 * b1 + b2


def _scan_segment(a, b, h0):
    A, Bc = lax.associative_scan(_lru_combine, (a, b), axis=1)
    return A * h0[:, None, :] + Bc


def rg_lru(x, w_a, b_a, w_x, b_x, lam):
    B, T, _ = x.shape
    xb = x.reshape(B, T, LRU_BLOCKS, LRU_BS)
    r = jax.nn.sigmoid(jnp.einsum('btni,nij->btnj', xb, w_a).reshape(B, T, D_RNN) + b_a)
    i = jax.nn.sigmoid(jnp.einsum('btni,nij->btnj', xb, w_x).reshape(B, T, D_RNN) + b_x)
    log_a = (-LRU_C * jax.nn.softplus(-lam.astype(jnp.float32))) * r.astype(jnp.float32)
    a = jnp.exp(log_a)
    bvec = jnp.sqrt(-jnp.expm1(2.0 * log_a)) * (i * x).astype(jnp.float32)
    h_meta = _scan_segment(a[:, :N_META], bvec[:, :N_META], jnp.zeros((B, D_RNN), jnp.float32))
    n_chunks = (T - N_META) // CHUNK
    a_c = a[:, N_META:].reshape(B, n_chunks, CHUNK, D_RNN).transpose(1, 0, 2, 3)
    b_c = bvec[:, N_META:].reshape(B, n_chunks, CHUNK, D_RNN).transpose(1, 0, 2, 3)

    def step(h, ab):
        hc = _scan_segment(ab[0], ab[1], h)
        return hc[:, -1], hc

    _, hs = lax.scan(step, h_meta[:, -1], (a_c, b_c))
    h_real = hs.transpose(1, 0, 2, 3).reshape(B, T - N_META, D_RNN)
    return jnp.concatenate([h_meta, h_real], axis=1).astype(x.dtype)


def forgetting_attention(q, k, v, f_logit):
    B, T, _ = q.shape
    q = q.reshape(B, T, N_HEADS, HEAD_DIM).transpose(0, 2, 1, 3)
    k = k.reshape(B, T, N_HEADS, HEAD_DIM).transpose(0, 2, 1, 3)
    v = v.reshape(B, T, N_HEADS, HEAD_DIM).transpose(0, 2, 1, 3)
    log_f = jax.nn.log_sigmoid(f_logit.astype(jnp.float32))
    F = jnp.cumsum(log_f, axis=1).transpose(0, 2, 1)
    scale = HEAD_DIM ** -0.5
    pos = jnp.arange(T)
    outs = []
    for start in range(0, T, Q_BLOCK):
        end = min(start + Q_BLOCK, T)
        s = jnp.einsum('bhqd,bhkd->bhqk', q[:, :, start:end], k[:, :, :end]).astype(jnp.float32) * scale
        s = s + (F[:, :, start:end, None] - F[:, :, None, :end])
        mask = pos[start:end, None] >= pos[None, :end]
        s = jnp.where(mask, s, -jnp.inf)
        p = jax.nn.softmax(s, axis=-1).astype(v.dtype)
        outs.append(jnp.einsum('bhqk,bhkd->bhqd', p, v[:, :, :end]))
    o = jnp.concatenate(outs, axis=2)
    return o.transpose(0, 2, 1, 3).reshape(B, T, D_ATTN)


def hybrid_mixer(u, w_in, conv_w, conv_b, lru_w_a, lru_b_a, lru_w_x, lru_b_x, lru_lambda,
                 forget_b, w_rnn_out, w_attn_out, w_o):
    z = u @ w_in
    idx, acc = [], 0
    for s in SPLITS[:-1]:
        acc += s
        idx.append(acc)
    xr, gr, q, k, v, fl, g_rnn, g_attn = jnp.split(z, idx, axis=-1)
    xc = causal_depthwise_conv(xr, conv_w, conv_b)
    y_rnn = rg_lru(xc, lru_w_a, lru_b_a, lru_w_x, lru_b_x, lru_lambda) * jax.nn.gelu(gr)
    y_attn = forgetting_attention(q, k, v, fl + forget_b)
    merged = jax.nn.sigmoid(g_rnn) * (y_rnn @ w_rnn_out) + jax.nn.sigmoid(g_attn) * (y_attn @ w_attn_out)
    return merged @ w_o


def setup_inputs(seed: int = 0) -> dict:
    key = jax.random.key(seed)
    ks = jax.random.split(key, 20)
    f32 = jnp.float32
    nrm = lambda k, shp, sc: jax.random.normal(k, shp, f32) * sc
    a0 = jax.random.uniform(ks[13], (DEPTH, D_RNN), f32, 0.9, 0.999)
    s0 = a0 ** (1.0 / LRU_C)
    lam = jnp.log(s0) - jnp.log1p(-s0)
    return {
        "x": nrm(ks[0], (BATCH, SEQ, D_MODEL), 1.0),
        "meta_tokens": nrm(ks[1], (N_META, D_MODEL), 1.0),
        "norm_g": 1.0 + nrm(ks[2], (DEPTH, N_NORMS, D_MODEL), 0.05),
        "ffn1_w_gu": nrm(ks[3], (DEPTH, D_MODEL, 2 * D_FF), D_MODEL ** -0.5),
        "ffn1_w_down": nrm(ks[4], (DEPTH, D_FF, D_MODEL), D_FF ** -0.5),
        "w_in": nrm(ks[5], (DEPTH, D_MODEL, N_IN), D_MODEL ** -0.5),
        "conv_w": nrm(ks[6], (DEPTH, CONV_W, D_RNN), CONV_W ** -0.5),
        "conv_b": nrm(ks[7], (DEPTH, D_RNN), 0.01),
        "lru_w_a": nrm(ks[8], (DEPTH, LRU_BLOCKS, LRU_BS, LRU_BS), LRU_BS ** -0.5),
        "lru_b_a": nrm(ks[9], (DEPTH, D_RNN), 0.01),
        "lru_w_x": nrm(ks[10], (DEPTH, LRU_BLOCKS, LRU_BS, LRU_BS), LRU_BS ** -0.5),
        "lru_b_x": nrm(ks[11], (DEPTH, D_RNN), 0.01),
        "lru_lambda": lam,
        "forget_b": jax.random.uniform(ks[12], (DEPTH, N_HEADS), f32, 2.0, 6.0),
        "w_rnn_out": nrm(ks[14], (DEPTH, D_RNN, D_MODEL), D_RNN ** -0.5),
        "w_attn_out": nrm(ks[15], (DEPTH, D_ATTN, D_MODEL), D_ATTN ** -0.5),
        "w_o": nrm(ks[16], (DEPTH, D_MODEL, D_MODEL), D_MODEL ** -0.5),
        "ffn2_w_gu": nrm(ks[17], (DEPTH, D_MODEL, 2 * D_FF), D_MODEL ** -0.5),
        "ffn2_w_down": nrm(ks[18], (DEPTH, D_FF, D_MODEL), D_FF ** -0.5),
    }


def reference(x, meta_tokens, norm_g, ffn1_w_gu, ffn1_w_down, w_in, conv_w, conv_b,
              lru_w_a, lru_b_a, lru_w_x, lru_b_x, lru_lambda, forget_b,
              w_rnn_out, w_attn_out, w_o, ffn2_w_gu, ffn2_w_down):
    B = x.shape[0]
    meta = jnp.broadcast_to(meta_tokens.astype(x.dtype)[None], (B, N_META, D_MODEL))
    h = jnp.concatenate([meta, x], axis=1)
    for l in range(DEPTH):
        g = norm_g[l]
        u = rms_norm(h, g[0])
        h = h + 0.5 * rms_norm(swiglu(u, ffn1_w_gu[l], ffn1_w_down[l]), g[1])
        u = rms_norm(h, g[2])
        mix = hybrid_mixer(u, w_in[l], conv_w[l], conv_b[l], lru_w_a[l], lru_b_a[l],
                           lru_w_x[l], lru_b_x[l], lru_lambda[l], forget_b[l],
                           w_rnn_out[l], w_attn_out[l], w_o[l])
        h = h + rms_norm(mix, g[3])
        u = rms_norm(h, g[4])
        h = h + 0.5 * rms_norm(swiglu(u, ffn2_w_gu[l], ffn2_w_down[l]), g[5])
    return h[:, N_META:]
```

```python
import numpy as np
from contextlib import ExitStack
import concourse.bass as bass
import concourse.mybir as mybir
from concourse.bass_utils import run_bass_kernel_spmd

F32 = mybir.dt.float32
BF16 = mybir.dt.bfloat16
U8 = mybir.dt.uint8
AF = mybir.ActivationFunctionType
ALU = mybir.AluOpType

D = 2048
NCH = 16
SEQ = 2048
NMETA = 16
T = SEQ + NMETA
DFF = 5632
NFC = 44
NH = 16
NIN = 14352
EPS = 1e-6
NPASS = 4
PW = T // NPASS
SEGW = PW // 2
SCALE = 128 ** -0.5
O_XR, O_GR, O_Q, O_K, O_V, O_FL, O_GRNN, O_GATT = 0, 2048, 4096, 6144, 8192, 10240, 10256, 12304
C_G, C_CW, C_CB, C_BA, C_BX, C_LAM, C_FB, NCOLS = 0, 96, 160, 176, 192, 208, 224, 256

COMPUTE = ("pe", "act", "dve", "pool")


class Res:
    __slots__ = ("name", "w", "rs", "rd")

    def __init__(self, name):
        self.name = name
        self.w = None
        self.rs = {}
        self.rd = []


class Op:
    __slots__ = ("eng", "fn", "deps", "idx", "need_inc", "semval", "is_dma", "dsem", "dval", "raw_same")

    def __init__(self, eng, fn, is_dma):
        self.eng = eng
        self.fn = fn
        self.deps = []
        self.idx = -1
        self.need_inc = False
        self.semval = 0
        self.is_dma = is_dma
        self.dsem = None
        self.dval = 0
        self.raw_same = []


class Sched:
    def __init__(self, K=8):
        self.ops = {e: [] for e in ("pe", "act", "dve", "pool", "sp")}
        self.K = K
        self.all_ops = []
        self.fence = {e: [] for e in self.ops}
        self.live_dma = []

    def _add(self, eng, fn, reads, writes, is_dma):
        op = Op(eng, fn, is_dma)
        cand = []
        for r in reads:
            if r.w is not None:
                cand.append((r.w, True))
        for w in writes:
            if w.w is not None:
                cand.append((w.w, False))
            for rd in w.rs.values():
                cand.append((rd, False))
            for rd in w.rd:
                cand.append((rd, False))
        for d in self.fence[eng]:
            cand.append((d, False))
        self.fence[eng] = []
        best = {}
        seen = set()
        for d, raw in cand:
            if d is op:
                continue
            if d.is_dma or is_dma:
                if id(d) not in seen:
                    seen.add(id(d))
                    if not (d.eng == eng and not d.is_dma):
                        op.deps.append(d)
                continue
            if d.eng == eng:
                if raw and eng != "pe":
                    if id(d) not in seen:
                        seen.add(id(d))
                        op.raw_same.append(d)
                continue
            b = best.get(d.eng)
            if b is None or d.idx > b.idx:
                best[d.eng] = d
        op.deps.extend(best.values())
        for r in reads:
            if is_dma:
                r.rd.append(op)
            else:
                r.rs[eng] = op
        for w in writes:
            w.w = op
            w.rs = {}
            w.rd = []
        op.idx = len(self.ops[eng])
        self.ops[eng].append(op)
        self.all_ops.append(op)
        if is_dma:
            self.live_dma.append(op)
        return op

    def op(self, eng, fn, reads=(), writes=()):
        return self._add(eng, fn, reads, writes, False)

    def dma(self, eng, fn, reads=(), writes=()):
        return self._add(eng, fn, reads, writes, True)

    def barrier(self):
        lasts = []
        for e in COMPUTE:
            for op in reversed(self.ops[e]):
                if not op.is_dma:
                    lasts.append(op)
                    break
        lasts.extend(self.live_dma)
        self.live_dma = []
        for e in self.fence:
            self.fence[e] = list(lasts)

    def emit(self, block, sems, dsems):
        for e in ("sp", "act", "pool"):
            n = 0
            for op in self.ops[e]:
                if op.is_dma:
                    op.dsem = dsems[e][n % self.K]
                    op.dval = 16 * (n // self.K + 1)
                    n += 1
        for op in self.all_ops:
            for d in op.deps:
                if not d.is_dma:
                    d.need_inc = True
            for d in op.raw_same:
                if op.idx - d.idx <= 3:
                    d.need_inc = True
        for e in COMPUTE:
            cnt = 0
            for op in self.ops[e]:
                if op.is_dma:
                    continue
                if op.need_inc:
                    cnt += 1
                    op.semval = cnt
        engobj = {"pe": "tensor", "act": "scalar", "dve": "vector", "pool": "gpsimd", "sp": "sync"}

        def run(ename, eng):
            waited = {}

            def wait(sem, val):
                k = id(sem)
                if waited.get(k, 0) >= val:
                    return
                waited[k] = val
                eng.wait_ge(sem, val)

            finals = {}
            for op in self.ops[ename]:
                for d in op.deps:
                    if d.is_dma:
                        wait(d.dsem, d.dval)
                    else:
                        wait(sems[d.eng], d.semval)
                for d in op.raw_same:
                    if op.idx - d.idx <= 3:
                        wait(sems[ename], d.semval)
                if op.is_dma:
                    if op.dval > 16:
                        wait(op.dsem, op.dval - 16)
                    ins = op.fn(eng)
                    ins.then_inc(op.dsem, 16)
                    finals[id(op.dsem)] = (op.dsem, op.dval)
                else:
                    ins = op.fn(eng)
                    if op.need_inc:
                        ins.then_inc(sems[ename], 1)
            for sem, val in finals.values():
                wait(sem, val)

        for ename in ("sp", "pe", "act", "dve", "pool"):
            if self.ops[ename]:
                getattr(block, engobj[ename])(lambda eng, _e=ename: run(_e, eng))


class Arena:
    def __init__(self, ap, nbytes):
        self.ap = ap
        self.n = nbytes
        self.off = 0

    def mark(self):
        return self.off

    def release(self, m):
        self.off = m

    def alloc(self, dtype, shape):
        esz = 2 if dtype == BF16 else 4
        n = 1
        for s in shape:
            n *= s
        nb = (n * esz + 63) // 64 * 64
        assert self.off + nb <= self.n, f"SBUF arena overflow {self.off}+{nb}>{self.n}"
        v = self.ap[:, self.off:self.off + n * esz].bitcast(dtype)
        self.off += nb
        if len(shape) == 2:
            v = v.rearrange("p (a b) -> p a b", a=shape[0])
        elif len(shape) == 3:
            v = v.rearrange("p (a b c) -> p a b c", a=shape[0], b=shape[1])
        return v


class Ring:
    def __init__(self, A, S, name, dtype, shape, n):
        self.bufs = [(A.alloc(dtype, shape), Res(f"{name}{i}")) for i in range(n)]
        self.i = 0

    def next(self):
        b = self.bufs[self.i % len(self.bufs)]
        self.i += 1
        return b


def segs_of(c0, c1, w):
    out = []
    c = c0
    while c < c1:
        n = min(w, c1 - c)
        out.append((c, n))
        c += n
    return out


def build(debug=False, phases=("tin", "ffn1", "mix", "ffn2", "tout")):
    nc = bass.Bass("TRN2", target_bir_lowering=False)

    def din(name, shape, dt=F32):
        return nc.dram_tensor(name, list(shape), dt, kind="ExternalInput").ap()

    def dscr(name, shape, dt=F32):
        kind = "ExternalOutput" if debug else "Internal"
        return nc.dram_tensor(name, list(shape), dt, kind=kind).ap()

    x = din("x", [SEQ, D])
    meta = din("meta", [NMETA, D])
    cols_d = din("cols", [128, NCOLS])
    ident_d = din("ident", [128, 128])
    tri_d = din("tri", [128, 128])
    w_gu1 = din("w_gu1", [D, 2 * DFF])
    w_dn1 = din("w_dn1", [DFF, D])
    w_in = din("w_in", [D, NIN])
    lru_wa = din("lru_wa", [NCH, 128, 128])
    lru_wx = din("lru_wx", [NCH, 128, 128])
    w_ro = din("w_ro", [D, D])
    w_ao = din("w_ao", [D, D])
    w_o = din("w_o", [D, D])
    w_gu2 = din("w_gu2", [D, 2 * DFF])
    w_dn2 = din("w_dn2", [DFF, D])
    out = nc.dram_tensor("out", [SEQ, D], F32, kind="ExternalOutput").ap()

    hT0 = dscr("hT0", [D, T])
    hT1 = dscr("hT1", [D, T])
    hT2 = dscr("hT2", [D, T])
    hT3 = dscr("hT3", [D, T])
    u2s = dscr("u2s", [D, T], BF16)
    yas = dscr("yas", [D, T], BF16)
    yrs = dscr("yrs", [D, T], BF16)
    yscr = dscr("yscr", [D, T])

    ARENA_BYTES = 207 * 1024
    es = ExitStack()
    with es:
        arena_t = es.enter_context(nc.sbuf_tensor("arena", [128, ARENA_BYTES], U8))
        ps_t = es.enter_context(nc.psum_tensor("ps", [128, 8 * 512], F32))
        sems = {e: es.enter_context(nc.semaphore("s_" + e)) for e in COMPUTE}
        dsems = {e: [es.enter_context(nc.semaphore(f"d_{e}{i}")) for i in range(8)] for e in ("sp", "act", "pool")}
        block = es.enter_context(nc.Block())
        S = Sched()
        A = Arena(arena_t, ARENA_BYTES)
        bank = [ps_t[:, b * 512:(b + 1) * 512] for b in range(8)]
        rbank = [Res(f"bank{b}") for b in range(8)]

        cols = A.alloc(F32, [NCOLS])
        dcol = A.alloc(F32, [160])
        ident = A.alloc(F32, [128])
        identb = A.alloc(BF16, [128])
        trib = A.alloc(BF16, [128])
        trif = A.alloc(F32, [128])
        onesb = A.alloc(BF16, [128])
        ones16 = A.alloc(F32, [128])
        r_const = Res("const")
        S.dma("sp", lambda e: e.dma_start(out=cols, in_=cols_d), writes=[r_const])
        S.dma("sp", lambda e: e.dma_start(out=ident, in_=ident_d), writes=[r_const])
        S.dma("sp", lambda e: e.dma_start(out=trif, in_=tri_d), writes=[r_const])
        S.op("dve", lambda e: e.tensor_copy(out=trib, in_=trif), reads=[r_const], writes=[r_const])
        S.op("dve", lambda e: e.tensor_copy(out=identb, in_=ident), reads=[r_const], writes=[r_const])
        S.op("dve", lambda e: e.memset(onesb, 1.0), writes=[r_const])
        S.op("dve", lambda e: e.memset(ones16, 1.0), writes=[r_const])
        sqrtD = float(np.sqrt(D))
        for n in range(6):
            f = sqrtD * (0.5 if n in (1, 5) else 1.0)
            S.op("dve", lambda e, n=n, f=f: e.tensor_scalar(out=dcol[:, n * 16:(n + 1) * 16], in0=cols[:, C_G + n * 16:C_G + (n + 1) * 16],
                                                             scalar1=f, scalar2=None, op0=ALU.mult), reads=[r_const], writes=[r_const])
        S.op("act", lambda e: e.activation(out=dcol[:, 96:112], in_=cols[:, C_LAM:C_LAM + 16], func=AF.Sigmoid), reads=[r_const], writes=[r_const])
        S.op("act", lambda e: e.activation(out=dcol[:, 96:112], in_=dcol[:, 96:112], func=AF.Ln), reads=[r_const], writes=[r_const])
        S.op("dve", lambda e: e.tensor_scalar(out=dcol[:, 112:128], in0=dcol[:, 96:112], scalar1=16.0, scalar2=None, op0=ALU.mult), reads=[r_const], writes=[r_const])
        S.op("dve", lambda e: e.tensor_scalar(out=dcol[:, 96:112], in0=dcol[:, 96:112], scalar1=8.0, scalar2=None, op0=ALU.mult), reads=[r_const], writes=[r_const])
        base_mark = A.mark()

        def gs_col(n, c):
            return dcol[:, n * 16 + c:n * 16 + c + 1]

        pool_rr = [0]

        def load_slab(dst, rdst, W, r0, nk, c0, ncols):
            S.dma("pool", lambda e: e.dma_start(out=dst, in_=W[r0:r0 + nk * 128, c0:c0 + ncols].rearrange("(a p) c -> p a c", p=128)),
                  writes=[rdst])

        def rstd_from(ps_ap, n, dst, rdst, rps):
            S.op("act", lambda e: e.activation(out=dst, in_=ps_ap, func=AF.Sqrt, bias=float(D * EPS)), reads=[rps], writes=[rdst])
            S.op("dve", lambda e: e.reciprocal(out=dst, in_=dst), reads=[rdst], writes=[rdst])

        def pos_tiles():
            return segs_of(0, T, 128)

        if "tin" in phases:
            m = A.mark()
            xt_ring = Ring(A, S, "xt", F32, [D], 3)
            st_ring = Ring(A, S, "st", F32, [NCH, 128], 3)
            def tin_load(ti):
                p0, n = pos_tiles()[ti]
                xt, rxt = xt_ring.next()
                if ti == 0:
                    S.dma("sp", lambda e: e.dma_start(out=xt[0:NMETA, :], in_=meta), writes=[rxt])
                    S.dma("sp", lambda e: e.dma_start(out=xt[NMETA:128, :], in_=x[0:128 - NMETA, :]), writes=[rxt])
                else:
                    S.dma("sp", lambda e: e.dma_start(out=xt[0:n, :], in_=x[p0 - NMETA:p0 - NMETA + n, :]), writes=[rxt])
                return xt, rxt

            nxt = tin_load(0)
            for ti, (p0, n) in enumerate(pos_tiles()):
                xt, rxt = nxt
                if ti + 1 < len(pos_tiles()):
                    nxt = tin_load(ti + 1)
                st, rst = st_ring.next()
                for g in range(4):
                    b = (ti * 4 + g) % 8
                    for j in range(4):
                        c = g * 4 + j
                        S.op("pe", lambda e, b=b, j=j, c=c, xt=xt, n=n: e.transpose(out=bank[b][:, j * 128:j * 128 + n], in_=xt[0:n, c * 128:(c + 1) * 128], identity=ident[0:n, 0:n]),
                             reads=[rxt, r_const], writes=[rbank[b]])
                    eng = "act" if g % 2 == 0 else "dve"
                    if eng == "act":
                        S.op("act", lambda e, b=b, g=g, st=st, n=n: e.activation(out=st[:, g * 4:(g + 1) * 4, 0:n], in_=bank[b].rearrange("p (a b) -> p a b", a=4)[:, :, 0:n], func=AF.Copy),
                             reads=[rbank[b]], writes=[rst])
                    else:
                        S.op("dve", lambda e, b=b, g=g, st=st, n=n: e.tensor_copy(out=st[:, g * 4:(g + 1) * 4, 0:n], in_=bank[b].rearrange("p (a b) -> p a b", a=4)[:, :, 0:n]),
                             reads=[rbank[b]], writes=[rst])
                S.dma("sp", lambda e, st=st, p0=p0, n=n: e.dma_start(out=hT0[:, p0:p0 + n].rearrange("(a p) c -> p a c", p=128), in_=st[:, :, 0:n]),
                      reads=[rst], writes=[])
            S.barrier()
            A.release(m)

        def sumsq(src_fn, rsrc, ncols_segs, sq_ring, banks):
            for c in range(NCH):
                sq, rsq = sq_ring.next()
                for si, (c0, n) in enumerate(ncols_segs):
                    S.op("act", lambda e, c=c, c0=c0, n=n, sq=sq, si=si: e.activation(out=sq[:, si, 0:n], in_=src_fn(c, c0, n), func=AF.Square),
                         reads=[rsrc[c // 4] if isinstance(rsrc, list) else rsrc], writes=[rsq])
                for si, (c0, n) in enumerate(ncols_segs):
                    b = banks[si]
                    S.op("pe", lambda e, c=c, n=n, sq=sq, si=si, b=b: e.matmul(bank[b][:, 0:n], lhsT=onesb, rhs=sq[:, si, 0:n], start=(c == 0), stop=(c == NCH - 1)),
                         reads=[rsq, r_const], writes=[rbank[b]])

        def ffn_phase(hin, hout, w_gu, w_dn, n_pre, n_post, final_out=False):
            NP3, PW3, SG = 3, 688, 344
            segs = [(0, SG), (SG, SG)]
            m = A.mark()
            uTs = [A.alloc(BF16, [NCH, PW3]) for _ in range(2)]
            r_uTs = [Res("uTa"), Res("uTb")]
            act = A.alloc(BF16, [NFC, PW3])
            r_act = Res("act")
            rstd_pre = A.alloc(F32, [PW3])
            rstd_post = A.alloc(F32, [PW3])
            r_rpre, r_rpost = Res("rpre"), Res("rpost")
            hc_ring = Ring(A, S, "hc", F32, [PW3], 5)
            yc_ring = Ring(A, S, "yc", F32, [PW3], 3)
            pn_ring = Ring(A, S, "pn", F32, [PW3], 3)
            sq_ring = Ring(A, S, "sq", BF16, [PW3], 4)
            sl_ring = Ring(A, S, "sl", F32, [PW3], 3)
            w_ring = Ring(A, S, "wff", BF16, [NFC, 128], 4)
            r_ys = [Res(f"ys{c}") for c in range(NCH)]
            ot_ring = Ring(A, S, "ot", F32, [6, 128], 2) if final_out else None
            evn = [0]

            def evac(out_ap, in_ap, rin, rout):
                evn[0] += 1
                if evn[0] % 2 == 0:
                    S.op("act", lambda e: e.activation(out=out_ap, in_=in_ap, func=AF.Copy), reads=[rin], writes=[rout])
                else:
                    S.op("dve", lambda e: e.tensor_copy(out=out_ap, in_=in_ap), reads=[rin], writes=[rout])

            w_res2 = {id(r_): Res("wff_b") for (_, r_) in w_ring.bufs}
            gb = [0]

            def gbank():
                b = gb[0] % 6
                gb[0] += 1
                return b

            def stats_mm(sq, rsq, c):
                for si, (c0, n) in enumerate(segs):
                    S.op("pe", lambda e, si=si, c0=c0, n=n: e.matmul(bank[6 + si][:, 0:n], lhsT=onesb, rhs=sq[:, c0:c0 + n], start=(c == 0), stop=(c == NCH - 1)),
                         reads=[rsq, r_const], writes=[rbank[6 + si]])

            def rstd_step(dst, rdst):
                for si, (c0, n) in enumerate(segs):
                    rstd_from(bank[6 + si][:, 0:n], n, dst[:, c0:c0 + n], rdst, rbank[6 + si])

            def prenorm_steps(p):
                P0 = p * PW3
                uT, ruT = uTs[p % 2], r_uTs[p % 2]
                steps = []

                loaded = []
                order = list(range(NCH)) + list(range(NCH))

                def prefetch():
                    if len(loaded) < 2 and order:
                        c_ = order.pop(0)
                        hc, rhc = hc_ring.next()
                        S.dma("sp", lambda e: e.dma_start(out=hc, in_=hin[c_ * 128:(c_ + 1) * 128, P0:P0 + PW3]), writes=[rhc])
                        loaded.append((hc, rhc))

                def st_stats(c):
                    prefetch()
                    prefetch()
                    hc, rhc = loaded.pop(0)
                    sq, rsq = sq_ring.next()
                    S.op("act", lambda e: e.activation(out=sq, in_=hc, func=AF.Square), reads=[rhc], writes=[rsq])
                    prefetch()
                    return lambda: stats_mm(sq, rsq, c)

                def st_u(c):
                    prefetch()
                    hc, rhc = loaded.pop(0)
                    S.op("dve", lambda e: e.scalar_tensor_tensor(out=uT[:, c, :], in0=hc, scalar=gs_col(n_pre, c), in1=rstd_pre, op0=ALU.mult, op1=ALU.mult),
                         reads=[rhc, r_rpre, r_const], writes=[ruT])
                    prefetch()

                for c in range(NCH):
                    steps.append(lambda c=c: st_stats(c))
                steps.append(lambda: rstd_step(rstd_pre, r_rpre))
                for c in range(NCH):
                    steps.append(lambda c=c: st_u(c))
                return steps

            def gateup(p, f):
                uT, ruT = uTs[p % 2], r_uTs[p % 2]
                wsl, rws = w_ring.next()
                rws2 = w_res2[id(rws)]
                load_slab(wsl[:, 0:NCH, :], rws, w_gu, 0, NCH, f * 128, 128)
                load_slab(wsl[:, NCH:2 * NCH, :], rws2, w_gu, 0, NCH, DFF + f * 128, 128)
                bs = [gbank() for _ in range(4)]
                for wi in range(2):
                    rw_ = rws if wi == 0 else rws2
                    for si, (c0, n) in enumerate(segs):
                        b = bs[wi * 2 + si]
                        for k in range(NCH):
                            S.op("pe", lambda e, b=b, n=n, wi=wi, k=k, c0=c0: e.matmul(bank[b][:, 0:n], lhsT=wsl[:, wi * NCH + k, :], rhs=uT[:, k, c0:c0 + n], start=(k == 0), stop=(k == NCH - 1)),
                                 reads=[rw_, ruT], writes=[rbank[b]])
                sl, rsl = sl_ring.next()
                for si, (c0, n) in enumerate(segs):
                    bg, bu = bs[si], bs[2 + si]
                    S.op("act", lambda e, bg=bg, n=n, c0=c0: e.activation(out=sl[:, c0:c0 + n], in_=bank[bg][:, 0:n], func=AF.Silu),
                         reads=[rbank[bg]], writes=[rsl])
                    S.op("dve", lambda e, bu=bu, n=n, c0=c0: e.tensor_tensor(out=act[:, f, c0:c0 + n], in0=bank[bu][:, 0:n], in1=sl[:, c0:c0 + n], op=ALU.mult),
                         reads=[rbank[bu], rsl], writes=[r_act])

            def down_chunk(p, dc, pending):
                P0 = p * PW3
                wd, rwd = w_ring.next()
                rwd2 = w_res2[id(rwd)]
                S.dma("pool", lambda e: e.dma_start(out=wd, in_=w_dn[:, dc * 128:(dc + 1) * 128].rearrange("(a p) c -> p a c", p=128)), writes=[rwd, rwd2])
                bs = [gbank(), gbank()]
                for si, (c0, n) in enumerate(segs):
                    b = bs[si]
                    for k in range(NFC):
                        S.op("pe", lambda e, b=b, n=n, k=k, c0=c0: e.matmul(bank[b][:, 0:n], lhsT=wd[:, k, :], rhs=act[:, k, c0:c0 + n], start=(k == 0), stop=(k == NFC - 1)),
                             reads=[rwd, rwd2, r_act], writes=[rbank[b]])
                while pending:
                    pending.pop(0)()
                yc, ryc = yc_ring.next()
                for si, (c0, n) in enumerate(segs):
                    b = bs[si]
                    S.op("act", lambda e, b=b, n=n, c0=c0: e.activation(out=yc[:, c0:c0 + n], in_=bank[b][:, 0:n], func=AF.Copy),
                         reads=[rbank[b]], writes=[ryc])
                sq, rsq = sq_ring.next()
                S.op("act", lambda e: e.activation(out=sq, in_=yc, func=AF.Square), reads=[ryc], writes=[rsq])
                pending.append(lambda: stats_mm(sq, rsq, dc))
                S.dma("sp", lambda e: e.dma_start(out=yscr[dc * 128:(dc + 1) * 128, P0:P0 + PW3], in_=yc), reads=[ryc], writes=[r_ys[dc]])

            def postnorm_steps(p):
                P0 = p * PW3
                steps = [lambda: rstd_step(rstd_post, r_rpost)]

                ld = {}

                def loads(c):
                    if c >= NCH or c in ld:
                        return
                    pn, rpn = pn_ring.next()
                    S.dma("sp", lambda e: e.dma_start(out=pn, in_=yscr[c * 128:(c + 1) * 128, P0:P0 + PW3]), reads=[r_ys[c]], writes=[rpn])
                    hc, rhc = hc_ring.next()
                    S.dma("sp", lambda e: e.dma_start(out=hc, in_=hin[c * 128:(c + 1) * 128, P0:P0 + PW3]), writes=[rhc])
                    ld[c] = (pn, rpn, hc, rhc)

                def st(c):
                    loads(c)
                    loads(c + 1)
                    pn, rpn, hc, rhc = ld.pop(c)
                    S.op("dve", lambda e: e.scalar_tensor_tensor(out=pn, in0=pn, scalar=gs_col(n_post, c), in1=rstd_post, op0=ALU.mult, op1=ALU.mult),
                         reads=[rpn, r_rpost, r_const], writes=[rpn])
                    S.op("dve", lambda e: e.tensor_tensor(out=pn, in0=pn, in1=hc, op=ALU.add), reads=[rpn, rhc], writes=[rpn])
                    if not final_out:
                        S.dma("sp", lambda e: e.dma_start(out=hout[c * 128:(c + 1) * 128, P0:P0 + PW3], in_=pn), reads=[rpn], writes=[])
                        return
                    return lambda: out_part(c, pn, rpn)

                def out_part(c, pn, rpn):
                    ot, rot = ot_ring.next()
                    blks = segs_of(0, PW3, 128)
                    for g0 in range(0, len(blks), 4):
                        b = gbank()
                        grp = blks[g0:g0 + 4]
                        for j, (q0_, n_) in enumerate(grp):
                            S.op("pe", lambda e, b=b, j=j, q0_=q0_, n_=n_: e.transpose(out=bank[b][0:n_, j * 128:(j + 1) * 128], in_=pn[:, q0_:q0_ + n_], identity=ident),
                                 reads=[rpn, r_const], writes=[rbank[b]])
                        nfull = sum(1 for (_, n_) in grp if n_ == 128)
                        if nfull:
                            evac(ot[:, g0:g0 + nfull, :], bank[b][:, 0:nfull * 128].rearrange("p (a b) -> p a b", b=128), rbank[b], rot)
                        for j, (q0_, n_) in enumerate(grp):
                            if n_ < 128:
                                evac(ot[0:n_, g0 + j, :], bank[b][0:n_, j * 128:(j + 1) * 128], rbank[b], rot)
                    bi = 0
                    if P0 == 0:
                        S.dma("sp", lambda e: e.dma_start(out=out[0:128 - NMETA, c * 128:(c + 1) * 128], in_=ot[NMETA:128, 0, :]), reads=[rot], writes=[])
                        bi = 1
                    nfb = sum(1 for (_, n_) in blks if n_ == 128)
                    r0 = P0 + bi * 128 - NMETA
                    S.dma("sp", lambda e: e.dma_start(out=out[r0:r0 + (nfb - bi) * 128, c * 128:(c + 1) * 128].rearrange("(b p) d -> p b d", p=128), in_=ot[:, bi:nfb, :]),
                          reads=[rot], writes=[])
                    for j, (q0_, n_) in enumerate(blks):
                        if n_ < 128:
                            r1 = P0 + q0_ - NMETA
                            S.dma("sp", lambda e, j=j, n_=n_, r1=r1: e.dma_start(out=out[r1:r1 + n_, c * 128:(c + 1) * 128], in_=ot[0:n_, j, :]), reads=[rot], writes=[])

                for c in range(NCH):
                    steps.append(lambda c=c: st(c))
                return steps

            late = []

            def run_step(stp):
                r_ = stp()
                if r_ is not None:
                    late.append(r_)

            def run_late():
                while late:
                    late.pop(0)()

            for stp in prenorm_steps(0):
                run_step(stp)
                run_late()
            post = []
            for p in range(NP3):
                pre = prenorm_steps(p + 1) if p + 1 < NP3 else []
                for f in range(NFC):
                    gateup(p, f)
                    run_late()
                    if post:
                        run_step(post.pop(0))
                    if f >= 6 and pre:
                        run_step(pre.pop(0))
                run_late()
                while pre:
                    run_step(pre.pop(0))
                    run_late()
                while post:
                    run_step(post.pop(0))
                    run_late()
                pending = []
                for dc in range(NCH):
                    down_chunk(p, dc, pending)
                while pending:
                    pending.pop(0)()
                post = postnorm_steps(p)
            while post:
                run_step(post.pop(0))
                run_late()
            S.barrier()
            A.release(m)

        def mix_phase():
            mtop = A.mark()
            segs2 = [(0, SEGW), (SEGW, SEGW)]
            u2 = A.alloc(BF16, [NCH, T])
            r_u2 = Res("u2")
            Fcol = A.alloc(F32, [17, 16])
            NQS = 9
            FrefB = A.alloc(F32, [16, NQS])
            r_F = Res("Fstuff")
            pj = [0]

            def pbank():
                b = 6 + (pj[0] % 2)
                pj[0] += 1
                return b

            ev = [0]

            def evac(out_ap, in_ap, rin, rout):
                ev[0] += 1
                if ev[0] % 2 == 0:
                    S.op("act", lambda e: e.activation(out=out_ap, in_=in_ap, func=AF.Copy), reads=[rin], writes=[rout])
                else:
                    S.op("dve", lambda e: e.tensor_copy(out=out_ap, in_=in_ap), reads=[rin], writes=[rout])

            m0 = A.mark()
            hT_ring = Ring(A, S, "hTm", F32, [NCH, PW], 2)
            rstd = A.alloc(F32, [2, SEGW])
            r_rstd = Res("rstdm")
            sq_ring = Ring(A, S, "sqm", BF16, [2, SEGW], 4)

            m0_res = {}

            def m0_load(p):
                hT_, r_hT_ = hT_ring.next()
                rl = m0_res.setdefault(id(r_hT_), [Res("hTm_a"), Res("hTm_b"), Res("hTm_c"), Res("hTm_d")])
                for g in range(4):
                    S.dma("sp", lambda e, g=g: e.dma_start(out=hT_[:, 4 * g:4 * g + 4, :], in_=hT1[g * 512:(g + 1) * 512, p * PW:(p + 1) * PW].rearrange("(a p) c -> p a c", p=128)),
                          writes=[rl[g]])
                return hT_, rl

            nxt0 = m0_load(0)
            for p in range(NPASS):
                P0 = p * PW
                hT, r_hT = nxt0
                if p + 1 < NPASS:
                    nxt0 = m0_load(p + 1)
                sumsq(lambda c, c0, n, hT=hT: hT[:, c, c0:c0 + n], r_hT, segs2, sq_ring, [4, 5])
                r_hT_l = r_hT
                for si, (c0, n) in enumerate(segs2):
                    rstd_from(bank[4 + si][:, 0:n], n, rstd[:, si, 0:n], r_rstd, rbank[4 + si])
                for c in range(NCH):
                    for si, (c0, n) in enumerate(segs2):
                        S.op("dve", lambda e, c=c, c0=c0, n=n, si=si, P0=P0, hT=hT: e.scalar_tensor_tensor(out=u2[:, c, P0 + c0:P0 + c0 + n], in0=hT[:, c, c0:c0 + n], scalar=gs_col(2, c),
                                                                                                   in1=rstd[:, si, 0:n], op0=ALU.mult, op1=ALU.mult),
                             reads=[r_hT_l[c // 4], r_rstd, r_const], writes=[r_u2])
            S.barrier()
            A.release(m0)

            mF = A.mark()
            wfl = A.alloc(BF16, [NCH, 16])
            r_wfl = Res("wfl")
            load_slab(wfl, r_wfl, w_in, 0, NCH, O_FL, 16)
            Ff = A.alloc(F32, [T])
            lf = A.alloc(F32, [T])
            onesT = A.alloc(F32, [T])
            Fend = A.alloc(F32, [NQS])
            rhsBD = A.alloc(F32, [16, NQS])
            r_Ff, r_lf, r_on = Res("Ff"), Res("lf"), Res("onesT")
            def F1():
                S.op("dve", lambda e: e.memset(onesT[0:16, :], 1.0), writes=[r_on])
                for (c0, n) in segs_of(0, T, 512):
                    b = pbank()
                    for k in range(NCH):
                        S.op("pe", lambda e, b=b, n=n, k=k, c0=c0: e.matmul(bank[b][0:16, 0:n], lhsT=wfl[:, k, :], rhs=u2[:, k, c0:c0 + n], start=(k == 0), stop=(k == NCH - 1)),
                             reads=[r_wfl, r_u2], writes=[rbank[b]])
                    S.op("act", lambda e, b=b, n=n, c0=c0: e.activation(out=lf[0:16, c0:c0 + n], in_=bank[b][0:16, 0:n], func=AF.Sigmoid, bias=cols[0:16, C_FB:C_FB + 1]),
                         reads=[rbank[b], r_const], writes=[r_lf])
                S.op("act", lambda e: e.activation(out=lf[0:16, :], in_=lf[0:16, :], func=AF.Ln), reads=[r_lf], writes=[r_lf])
                S.op("dve", lambda e: e.tensor_tensor_scan(out=Ff[0:16, :], data0=onesT[0:16, :], data1=lf[0:16, :], initial=0.0, op0=ALU.mult, op1=ALU.add),
                     reads=[r_on, r_lf], writes=[r_Ff])

            def F2():
                for i, (p0, n) in enumerate(pos_tiles()):
                    b = pbank()
                    S.op("pe", lambda e, b=b, p0=p0, n=n: e.matmul(bank[b][0:n, 0:16], lhsT=Ff[0:16, p0:p0 + n], rhs=ident[0:16, 0:16], start=True, stop=True),
                         reads=[r_Ff, r_const], writes=[rbank[b]])
                    S.op("dve", lambda e, b=b, i=i, n=n: e.tensor_copy(out=Fcol[0:n, i, :], in_=bank[b][0:n, 0:16]), reads=[rbank[b]], writes=[r_F])
                S.op("dve", lambda e: e.tensor_copy(out=Fend[0:16, 0:8], in_=Ff[0:16, 0:2048].rearrange("p (a b) -> p a b", b=256)[:, :, 255]), reads=[r_Ff], writes=[r_lf])
                S.op("dve", lambda e: e.tensor_copy(out=Fend[0:16, 8:9], in_=Ff[0:16, T - 1:T]), reads=[r_Ff], writes=[r_lf])
                for h in range(NH):
                    S.op("dve", lambda e, h=h: e.tensor_scalar(out=rhsBD[0:16, h, :], in0=Fend[0:16, :], scalar1=ident[0:16, h:h + 1], scalar2=None, op0=ALU.mult),
                         reads=[r_lf, r_const], writes=[r_on])
                b = pbank()
                S.op("pe", lambda e, b=b: e.matmul(bank[b][:, 0:16 * NQS], lhsT=ones16[0:16, :], rhs=rhsBD[0:16].rearrange("p a b -> p (a b)"), start=True, stop=True),
                     reads=[r_on, r_const], writes=[rbank[b]])
                S.op("dve", lambda e, b=b: e.tensor_copy(out=FrefB.rearrange("p a b -> p (a b)"), in_=bank[b][:, 0:16 * NQS]), reads=[rbank[b]], writes=[r_F])


            mA = A.mark()
            wqk_ring = Ring(A, S, "wqk", BF16, [NCH, 128], 4)
            wv_ring = Ring(A, S, "wv", BF16, [NCH, 512], 1)
            qk_ring = Ring(A, S, "qk", BF16, [T], 4)
            V_ring = Ring(A, S, "V", BF16, [17, 512], 2)
            PT_ring = Ring(A, S, "PT", BF16, [512], 4)
            bc_ring = Ring(A, S, "bc", F32, [17, NQS], 2)
            dn_ring = Ring(A, S, "dn", F32, [512], 2)
            ya_ring = Ring(A, S, "ya", BF16, [T], 2)
            ktiles = pos_tiles()
            qblocks = segs_of(0, T, 512)
            sb = [0]

            def attention(h, hl, qT, rq, kT, rk, V, rV, ya, rya):
                tasks = []
                for j, (q0, qw) in enumerate(qblocks):
                    tl = [i for i, (k0, kn) in enumerate(ktiles) if k0 < q0 + qw]
                    for idx, i in enumerate(tl):
                        tasks.append((j, q0, qw, i, idx == 0, idx == len(tl) - 1))
                st = {}
                bc, rbc = bc_ring.next()
                for i, (k0, kn) in enumerate(ktiles):
                    S.op("dve", lambda e, i=i, kn=kn: e.tensor_scalar(out=bc[0:kn, i, :], in0=FrefB[0:kn, h, :], scalar1=Fcol[0:kn, i, h:h + 1], scalar2=None, op0=ALU.subtract),
                         reads=[r_F], writes=[rbc])

                def s1(t):
                    j, q0, qw, i, first, last = tasks[t]
                    k0, kn = ktiles[i]
                    qc0 = max(0, k0 - q0)
                    N = qw - qc0
                    diag = k0 >= q0
                    bs_ = sb[0] % 2
                    sb[0] += 1
                    S.op("pe", lambda e: e.matmul(bank[bs_][0:kn, 0:N], lhsT=kT[:, k0:k0 + kn], rhs=qT[:, q0 + qc0:q0 + qw], start=True, stop=True),
                         reads=[rq, rk], writes=[rbank[bs_]])
                    PT, rPT = PT_ring.next()
                    pa = q0 + qc0
                    while pa < q0 + qw:
                        pb = min((pa // 256 + 1) * 256, q0 + qw)
                        s0, sn, qs = pa - q0 - qc0, pb - pa, pa // 256
                        S.op("act", lambda e, s0=s0, sn=sn, qs=qs: e.activation(out=PT[0:kn, s0:s0 + sn], in_=bank[bs_][0:kn, s0:s0 + sn], func=AF.Exp,
                                                                              bias=bc[0:kn, i, qs:qs + 1], scale=float(SCALE)),
                             reads=[rbank[bs_], rbc], writes=[rPT])
                        pa = pb
                    if diag:
                        dn = min(128, N)
                        S.op("dve", lambda e: e.tensor_tensor(out=PT[0:kn, 0:dn], in0=PT[0:kn, 0:dn], in1=trib[0:kn, 0:dn], op=ALU.mult),
                             reads=[rPT, r_const], writes=[rPT])
                    st[t] = (PT, rPT, qc0, N)

                def s2(t):
                    j, q0, qw, i, first, last = tasks[t]
                    k0, kn = ktiles[i]
                    PT, rPT, qc0, N = st.pop(t)
                    bo, bd = 2 + (j % 2), 4 + (j % 2)
                    S.op("pe", lambda e: e.matmul(bank[bo][:, qc0:qw], lhsT=V[0:kn, i, hl * 128:(hl + 1) * 128], rhs=PT[0:kn, 0:N], start=first, stop=last),
                         reads=[rV, rPT], writes=[rbank[bo]])
                    S.op("pe", lambda e: e.matmul(bank[bd][:, qc0:qw], lhsT=onesb[0:kn, :], rhs=PT[0:kn, 0:N], start=first, stop=last),
                         reads=[r_const, rPT], writes=[rbank[bd]])
                    if last:
                        rd, rrd = dn_ring.next()
                        S.op("dve", lambda e: e.reciprocal(out=rd[:, 0:qw], in_=bank[bd][:, 0:qw]), reads=[rbank[bd]], writes=[rrd])
                        S.op("dve", lambda e: e.tensor_tensor(out=ya[:, q0:q0 + qw], in0=bank[bo][:, 0:qw], in1=rd[:, 0:qw], op=ALU.mult),
                             reads=[rbank[bo], rrd], writes=[rya])

                steps = [lambda: s1(0)]

                def mk(t):
                    def f_():
                        if t + 1 < len(tasks):
                            s1(t + 1)
                        s2(t)
                    return f_
                for t in range(len(tasks)):
                    steps.append(mk(t))
                return steps

            Vcur = {}

            def proj_groups(h):
                groups = []
                hg, hl = divmod(h, 4)
                if hl == 0:
                    wv, rwv = wv_ring.next()
                    V, rV = V_ring.next()
                    Vcur[hg] = (V, rV)

                    def ldv():
                        load_slab(wv, rwv, w_in, 0, NCH, O_V + hg * 512, 512)
                    groups.append(ldv)
                    for i, (p0, n) in enumerate(ktiles):
                        def gv(i=i, p0=p0, n=n):
                            b = pbank()
                            for k in range(NCH):
                                S.op("pe", lambda e, b=b, k=k: e.matmul(bank[b][0:n, :], lhsT=u2[:, k, p0:p0 + n], rhs=wv[:, k, :], start=(k == 0), stop=(k == NCH - 1)),
                                     reads=[rwv, r_u2], writes=[rbank[b]])
                            evac(V[0:n, i, :], bank[b][0:n, :], rbank[b], rV)
                        groups.append(gv)
                wq, rwq = wqk_ring.next()
                wk, rwk = wqk_ring.next()
                qT, rq = qk_ring.next()
                kT, rk = qk_ring.next()

                def ldqk():
                    load_slab(wq, rwq, w_in, 0, NCH, O_Q + h * 128, 128)
                    load_slab(wk, rwk, w_in, 0, NCH, O_K + h * 128, 128)
                groups.append(ldqk)
                for (wt, rwt, dst, rdst) in ((wq, rwq, qT, rq), (wk, rwk, kT, rk)):
                    for (c0, n) in segs_of(0, T, 344):
                        def gq(wt=wt, rwt=rwt, dst=dst, rdst=rdst, c0=c0, n=n):
                            b = pbank()
                            for k in range(NCH):
                                S.op("pe", lambda e, b=b, k=k: e.matmul(bank[b][:, 0:n], lhsT=wt[:, k, :], rhs=u2[:, k, c0:c0 + n], start=(k == 0), stop=(k == NCH - 1)),
                                     reads=[rwt, r_u2], writes=[rbank[b]])
                            evac(dst[:, c0:c0 + n], bank[b][:, 0:n], rbank[b], rdst)
                        groups.append(gq)
                return groups, (qT, rq, kT, rk)

            g0, qk_next = proj_groups(0)
            F1()
            for gi_, g_ in enumerate(g0):
                if gi_ == 10:
                    F2()
                g_()
            for h in range(NH):
                hg, hl = divmod(h, 4)
                qT, rq, kT, rk = qk_next
                V, rV = Vcur[hg]
                ya, rya = ya_ring.next()
                asteps = attention(h, hl, qT, rq, kT, rk, V, rV, ya, rya)
                if h + 1 < NH:
                    pg, qk_next = proj_groups(h + 1)
                else:
                    pg = []
                na, npg = len(asteps), len(pg)
                ai = 0
                for gi_, g_ in enumerate(pg):
                    g_()
                    tgt = (gi_ + 1) * na // max(npg, 1)
                    while ai < tgt:
                        asteps[ai]()
                        ai += 1
                while ai < na:
                    asteps[ai]()
                    ai += 1
                S.dma("sp", lambda e, h=h, ya=ya: e.dma_start(out=yas[h * 128:(h + 1) * 128, :], in_=ya), reads=[rya], writes=[])
                if h < 4:
                    S.dma("sp", lambda e, h=h: e.dma_start(out=u2s[h * 512:(h + 1) * 512, :].rearrange("(a p) c -> p a c", p=128), in_=u2[:, 4 * h:4 * h + 4, :]),
                          reads=[r_u2], writes=[])
            S.barrier()
            A.release(mF)

            mR = A.mark()
            lwa = A.alloc(BF16, [NCH, 128])
            lwx = A.alloc(BF16, [NCH, 128])
            r_lw = Res("lw")
            S.dma("pool", lambda e: e.dma_start(out=lwa, in_=lru_wa.rearrange("n i j -> i n j")), writes=[r_lw])
            S.dma("pool", lambda e: e.dma_start(out=lwx, in_=lru_wx.rearrange("n i j -> i n j")), writes=[r_lw])
            wr_ring = Ring(A, S, "wr", BF16, [NCH, 128], 6)
            xr_ring = Ring(A, S, "xr", F32, [T + 3], 2)
            xc_ring = Ring(A, S, "xc", F32, [T], 1)
            xcb_ring = Ring(A, S, "xcb", BF16, [T], 1)
            rr_ring = Ring(A, S, "rr", F32, [T], 1)
            ii_ring = Ring(A, S, "ii", F32, [T], 1)
            a2_ring = Ring(A, S, "a2", F32, [T], 1)
            gg_ring = Ring(A, S, "gg", BF16, [T], 2)
            hr_ring = Ring(A, S, "hr", F32, [T], 1)
            yo_ring = Ring(A, S, "yo", BF16, [T], 2)
            segs5 = segs_of(0, T, 344)

            def pcol(base, c):
                return cols[:, base + c:base + c + 1]

            pjb = [0]
            gtb = [0]

            def pbank4():
                b = pjb[0] % 4
                pjb[0] += 1
                return b

            def gbank4():
                b = 4 + gtb[0] % 4
                gtb[0] += 1
                return b

            stt = {}

            def stageA(c):
                wxr, rwxr = wr_ring.next()
                load_slab(wxr, rwxr, w_in, 0, NCH, O_XR + c * 128, 128)
                wgr, rwgr = wr_ring.next()
                load_slab(wgr, rwgr, w_in, 0, NCH, O_GR + c * 128, 128)
                xr, rxr = xr_ring.next()
                S.op("dve", lambda e: e.memset(xr[:, 0:3], 0.0), writes=[rxr])
                for (c0, n) in segs5:
                    b = pbank4()
                    for k in range(NCH):
                        S.op("pe", lambda e, b=b, n=n, k=k, c0=c0: e.matmul(bank[b][:, 0:n], lhsT=wxr[:, k, :], rhs=u2[:, k, c0:c0 + n], start=(k == 0), stop=(k == NCH - 1)),
                             reads=[rwxr, r_u2], writes=[rbank[b]])
                    evac(xr[:, 3 + c0:3 + c0 + n], bank[b][:, 0:n], rbank[b], rxr)
                stt[c] = dict(wgr=wgr, rwgr=rwgr, xr=xr, rxr=rxr)

            def stageB(c):
                d_ = stt[c]
                xr, rxr = d_["xr"], d_["rxr"]
                xc, rxc = xc_ring.next()
                S.op("dve", lambda e: e.tensor_scalar(out=xc, in0=xr[:, 0:T], scalar1=pcol(C_CW, c), scalar2=pcol(C_CB, c), op0=ALU.mult, op1=ALU.add),
                     reads=[rxr, r_const], writes=[rxc])
                for kk in range(1, 4):
                    S.op("dve", lambda e, kk=kk: e.scalar_tensor_tensor(out=xc, in0=xr[:, kk:kk + T], scalar=pcol(C_CW + kk * 16, c), in1=xc, op0=ALU.mult, op1=ALU.add),
                         reads=[rxr, rxc, r_const], writes=[rxc])
                xcb, rxcb = xcb_ring.next()
                S.op("act", lambda e: e.activation(out=xcb, in_=xc, func=AF.Copy), reads=[rxc], writes=[rxcb])
                d_.update(xc=xc, rxc=rxc, xcb=xcb, rxcb=rxcb)

            def stageC(c):
                d_ = stt[c]
                xcb, rxcb = d_["xcb"], d_["rxcb"]
                rr, rrr = rr_ring.next()
                ii, rii = ii_ring.next()
                for (lw, dst, rdst, bcol) in ((lwa, rr, rrr, C_BA), (lwx, ii, rii, C_BX)):
                    for (c0, n) in segs5:
                        b = gbank4()
                        S.op("pe", lambda e, b=b, n=n, c0=c0, lw=lw: e.matmul(bank[b][:, 0:n], lhsT=lw[:, c, :], rhs=xcb[:, c0:c0 + n], start=True, stop=True),
                             reads=[r_lw, rxcb], writes=[rbank[b]])
                        S.op("act", lambda e, b=b, n=n, c0=c0, dst=dst, bcol=bcol: e.activation(out=dst[:, c0:c0 + n], in_=bank[b][:, 0:n], func=AF.Sigmoid, bias=pcol(bcol, c)),
                             reads=[rbank[b], r_const], writes=[rdst])
                d_.update(rr=rr, rrr=rrr, ii=ii, rii=rii)

            def stageD(c):
                d_ = stt[c]
                rr, rrr, ii, rii, xc, rxc = d_["rr"], d_["rrr"], d_["ii"], d_["rii"], d_["xc"], d_["rxc"]
                a2, ra2 = a2_ring.next()
                S.op("act", lambda e: e.activation(out=a2, in_=rr, func=AF.Exp, scale=dcol[:, 112 + c:113 + c]), reads=[rrr, r_const], writes=[ra2])
                S.op("act", lambda e: e.activation(out=rr, in_=rr, func=AF.Exp, scale=dcol[:, 96 + c:97 + c]), reads=[rrr, r_const], writes=[rrr])
                S.op("dve", lambda e: e.tensor_scalar(out=a2, in0=a2, scalar1=1.0, scalar2=-1.0, op0=ALU.min, op1=ALU.mult), reads=[ra2], writes=[ra2])
                S.op("act", lambda e: e.activation(out=a2, in_=a2, func=AF.Sqrt, bias=1.0), reads=[ra2], writes=[ra2])
                S.op("dve", lambda e: e.tensor_tensor(out=ii, in0=ii, in1=xc, op=ALU.mult), reads=[rii, rxc], writes=[rii])
                S.op("dve", lambda e: e.tensor_tensor(out=ii, in0=ii, in1=a2, op=ALU.mult), reads=[rii, ra2], writes=[rii])
                hr, rhr = hr_ring.next()
                S.op("dve", lambda e: e.tensor_tensor_scan(out=hr, data0=rr, data1=ii, initial=0.0, op0=ALU.mult, op1=ALU.add),
                     reads=[rrr, rii], writes=[rhr])
                d_.update(hr=hr, rhr=rhr)

            def stageE(c):
                d_ = stt.pop(c)
                wgr, rwgr, hr, rhr = d_["wgr"], d_["rwgr"], d_["hr"], d_["rhr"]
                gg, rgg = gg_ring.next()
                for (c0, n) in segs5:
                    b = pbank4()
                    for k in range(NCH):
                        S.op("pe", lambda e, b=b, n=n, k=k, c0=c0: e.matmul(bank[b][:, 0:n], lhsT=wgr[:, k, :], rhs=u2[:, k, c0:c0 + n], start=(k == 0), stop=(k == NCH - 1)),
                             reads=[rwgr, r_u2], writes=[rbank[b]])
                    S.op("act", lambda e, b=b, n=n, c0=c0: e.activation(out=gg[:, c0:c0 + n], in_=bank[b][:, 0:n], func=AF.Gelu_apprx_tanh),
                         reads=[rbank[b]], writes=[rgg])
                yo, ryo = yo_ring.next()
                S.op("dve", lambda e: e.tensor_tensor(out=yo, in0=hr, in1=gg, op=ALU.mult), reads=[rhr, rgg], writes=[ryo])
                S.dma("sp", lambda e: e.dma_start(out=yrs[c * 128:(c + 1) * 128, :], in_=yo), reads=[ryo], writes=[])

            stageA(0)
            for c in range(NCH):
                stageB(c)
                if c + 1 < NCH:
                    stageA(c + 1)
                stageC(c)
                stageD(c)
                stageE(c)
            S.barrier()
            A.release(mtop)

            m3 = A.mark()
            NP3, PW3, SG = 3, 688, 344
            segs3 = [(0, SG), (SG, SG)]
            u2t = A.alloc(BF16, [NCH, PW3])
            yrt = A.alloc(BF16, [NCH, PW3])
            yat = A.alloc(BF16, [NCH, PW3])
            mg = A.alloc(BF16, [NCH, PW3])
            rstd3 = A.alloc(F32, [PW3])
            r_u2t, r_yrt, r_yat, r_mg, r_rstd3 = Res("u2t"), Res("yrt"), Res("yat"), Res("mg"), Res("rstd3")
            w3_ring = Ring(A, S, "w3", BF16, [NCH, 128], 17)
            sg_ring = Ring(A, S, "sg", F32, [2, SG], 4)
            hc3_ring = Ring(A, S, "hc3", F32, [PW3], 4)
            yc3_ring = Ring(A, S, "yc3", F32, [PW3], 3)
            pn3_ring = Ring(A, S, "pn3", F32, [PW3], 4)
            sq3_ring = Ring(A, S, "sq3", BF16, [PW3], 4)
            r_ys3 = [Res(f"ys3_{c}") for c in range(NCH)]
            gb3 = [0]

            def gbank3():
                b = gb3[0] % 6
                gb3[0] += 1
                return b

            def stats3(sq, rsq, c):
                for si, (c0, n) in enumerate(segs3):
                    S.op("pe", lambda e, si=si, c0=c0, n=n: e.matmul(bank[6 + si][:, 0:n], lhsT=onesb, rhs=sq[:, c0:c0 + n], start=(c == 0), stop=(c == NCH - 1)),
                         reads=[rsq, r_const], writes=[rbank[6 + si]])

            def load_inputs3(p):
                P0 = p * PW3
                for (dst, rdst, src) in ((u2t, r_u2t, u2s), (yrt, r_yrt, yrs), (yat, r_yat, yas)):
                    S.dma("sp", lambda e, dst=dst, src=src: e.dma_start(out=dst, in_=src[:, P0:P0 + PW3].rearrange("(a p) c -> p a c", p=128)), writes=[rdst])

            def merge_chunk(p, dc):
                sl = []
                for (W, c0w) in ((w_in, O_GRNN + dc * 128), (w_in, O_GATT + dc * 128), (w_ro, dc * 128), (w_ao, dc * 128)):
                    wt, rwt = w3_ring.next()
                    load_slab(wt, rwt, W, 0, NCH, c0w, 128)
                    sl.append((wt, rwt))
                srcs = ((u2t, r_u2t), (u2t, r_u2t), (yrt, r_yrt), (yat, r_yat))
                for si, (c0, n) in enumerate(segs3):
                    bs = [gbank3() for _ in range(4)]
                    for gi in range(4):
                        wt, rwt = sl[gi]
                        sa, rsa = srcs[gi]
                        b = bs[gi]
                        for k in range(NCH):
                            S.op("pe", lambda e, b=b, n=n, k=k, c0=c0, wt=wt, sa=sa: e.matmul(bank[b][:, 0:n], lhsT=wt[:, k, :], rhs=sa[:, k, c0:c0 + n], start=(k == 0), stop=(k == NCH - 1)),
                                 reads=[rwt, rsa], writes=[rbank[b]])
                    sg, rsg = sg_ring.next()
                    for gi in range(2):
                        S.op("act", lambda e, gi=gi, b=bs[gi], n=n, sg=sg: e.activation(out=sg[:, gi, 0:n], in_=bank[b][:, 0:n], func=AF.Sigmoid),
                             reads=[rbank[bs[gi]]], writes=[rsg])
                    for gi in range(2):
                        S.op("dve", lambda e, gi=gi, b=bs[2 + gi], n=n, sg=sg: e.tensor_tensor(out=sg[:, gi, 0:n], in0=bank[b][:, 0:n], in1=sg[:, gi, 0:n], op=ALU.mult),
                             reads=[rbank[bs[2 + gi]], rsg], writes=[rsg])
                    S.op("dve", lambda e, n=n, sg=sg, c0=c0: e.tensor_tensor(out=mg[:, dc, c0:c0 + n], in0=sg[:, 0, 0:n], in1=sg[:, 1, 0:n], op=ALU.add),
                         reads=[rsg], writes=[r_mg])

            def wo_chunk(p, dc, pending):
                P0 = p * PW3
                wt, rwt = w3_ring.next()
                load_slab(wt, rwt, w_o, 0, NCH, dc * 128, 128)
                bs = [gbank3(), gbank3()]
                for si, (c0, n) in enumerate(segs3):
                    b = bs[si]
                    for k in range(NCH):
                        S.op("pe", lambda e, b=b, n=n, k=k, c0=c0: e.matmul(bank[b][:, 0:n], lhsT=wt[:, k, :], rhs=mg[:, k, c0:c0 + n], start=(k == 0), stop=(k == NCH - 1)),
                             reads=[rwt, r_mg], writes=[rbank[b]])
                while pending:
                    pending.pop(0)()
                yc, ryc = yc3_ring.next()
                for si, (c0, n) in enumerate(segs3):
                    b = bs[si]
                    S.op("act", lambda e, b=b, n=n, c0=c0: e.activation(out=yc[:, c0:c0 + n], in_=bank[b][:, 0:n], func=AF.Copy),
                         reads=[rbank[b]], writes=[ryc])
                sq, rsq = sq3_ring.next()
                S.op("act", lambda e: e.activation(out=sq, in_=yc, func=AF.Square), reads=[ryc], writes=[rsq])
                pending.append(lambda: stats3(sq, rsq, dc))
                S.dma("sp", lambda e: e.dma_start(out=yscr[dc * 128:(dc + 1) * 128, P0:P0 + PW3], in_=yc), reads=[ryc], writes=[r_ys3[dc]])

            def postnorm3(p):
                P0 = p * PW3

                def rs():
                    for si, (c0, n) in enumerate(segs3):
                        rstd_from(bank[6 + si][:, 0:n], n, rstd3[:, c0:c0 + n], r_rstd3, rbank[6 + si])
                steps = [rs]

                ld = {}

                def loads(c):
                    if c >= NCH or c in ld:
                        return
                    pn, rpn = pn3_ring.next()
                    S.dma("sp", lambda e: e.dma_start(out=pn, in_=yscr[c * 128:(c + 1) * 128, P0:P0 + PW3]), reads=[r_ys3[c]], writes=[rpn])
                    hc, rhc = hc3_ring.next()
                    S.dma("sp", lambda e: e.dma_start(out=hc, in_=hT1[c * 128:(c + 1) * 128, P0:P0 + PW3]), writes=[rhc])
                    ld[c] = (pn, rpn, hc, rhc)

                def st(c):
                    loads(c)
                    loads(c + 1)
                    pn, rpn, hc, rhc = ld.pop(c)
                    S.op("dve", lambda e: e.scalar_tensor_tensor(out=pn, in0=pn, scalar=gs_col(3, c), in1=rstd3, op0=ALU.mult, op1=ALU.mult),
                         reads=[rpn, r_rstd3, r_const], writes=[rpn])
                    S.op("dve", lambda e: e.tensor_tensor(out=pn, in0=pn, in1=hc, op=ALU.add), reads=[rpn, rhc], writes=[rpn])
                    S.dma("sp", lambda e: e.dma_start(out=hT2[c * 128:(c + 1) * 128, P0:P0 + PW3], in_=pn), reads=[rpn], writes=[])

                for c in range(NCH):
                    steps.append(lambda c=c: st(c))
                return steps

            load_inputs3(0)
            post = []
            for p in range(NP3):
                for dc in range(NCH):
                    merge_chunk(p, dc)
                    if post:
                        post.pop(0)()
                    if dc == 0 and post:
                        post.pop(0)()
                while post:
                    post.pop(0)()
                if p + 1 < NP3:
                    load_inputs3(p + 1)
                pending = []
                for dc in range(NCH):
                    wo_chunk(p, dc, pending)
                while pending:
                    pending.pop(0)()
                post = postnorm3(p)
            while post:
                post.pop(0)()
            S.barrier()
            A.release(m3)

        if "ffn1" in phases:
            ffn_phase(hT0, hT1, w_gu1, w_dn1, 0, 1)

        if "mix" in phases:
            mix_phase()

        fuse_out = "ffn2" in phases and "tout" in phases
        if "ffn2" in phases:
            ffn_phase(hT2 if "mix" in phases else hT1, hT3, w_gu2, w_dn2, 4, 5, final_out=fuse_out)

        if "tout" in phases and not fuse_out:
            m = A.mark()
            src = hT3 if "ffn2" in phases else (hT1 if "ffn1" in phases else hT0)
            st_ring = Ring(A, S, "st2", F32, [NCH, 128], 2)
            ot_ring = Ring(A, S, "ot", F32, [D], 2)
            for ti, (p0, n) in enumerate(pos_tiles()):
                st, rst = st_ring.next()
                ot, rot = ot_ring.next()
                S.dma("sp", lambda e, st=st, p0=p0, n=n: e.dma_start(out=st[:, :, 0:n], in_=src[:, p0:p0 + n].rearrange("(a p) c -> p a c", p=128)), writes=[rst])
                for g in range(4):
                    b = (ti * 4 + g) % 8
                    for j in range(4):
                        c = g * 4 + j
                        S.op("pe", lambda e, b=b, j=j, c=c, st=st, n=n: e.transpose(out=bank[b][0:n, j * 128:(j + 1) * 128], in_=st[:, c, 0:n], identity=ident),
                             reads=[rst, r_const], writes=[rbank[b]])
                    if g % 2 == 0:
                        S.op("act", lambda e, b=b, g=g, ot=ot, n=n: e.activation(out=ot[0:n, g * 512:(g + 1) * 512], in_=bank[b][0:n, :], func=AF.Copy),
                             reads=[rbank[b]], writes=[rot])
                    else:
                        S.op("dve", lambda e, b=b, g=g, ot=ot, n=n: e.tensor_copy(out=ot[0:n, g * 512:(g + 1) * 512], in_=bank[b][0:n, :]),
                             reads=[rbank[b]], writes=[rot])
                if ti == 0:
                    S.dma("sp", lambda e, ot=ot: e.dma_start(out=out[0:128 - NMETA, :], in_=ot[NMETA:128, :]), reads=[rot], writes=[])
                else:
                    S.dma("sp", lambda e, ot=ot, p0=p0, n=n: e.dma_start(out=out[p0 - NMETA:p0 - NMETA + n, :], in_=ot[0:n, :]), reads=[rot], writes=[])
            S.barrier()
            A.release(m)

        S.emit(block, sems, dsems)
    return nc


def host_consts(norm_g, conv_w, conv_b, lru_b_a, lru_b_x, lru_lambda, forget_b):
    cols = np.zeros((128, NCOLS), np.float32)

    def pc(v):
        return np.ascontiguousarray(np.asarray(v, np.float32).reshape(NCH, 128).T)

    for n in range(6):
        cols[:, C_G + n * 16:C_G + (n + 1) * 16] = pc(norm_g[0, n])
    for k in range(4):
        cols[:, C_CW + k * 16:C_CW + (k + 1) * 16] = pc(conv_w[0, k])
    cols[:, C_CB:C_CB + 16] = pc(conv_b[0])
    cols[:, C_BA:C_BA + 16] = pc(lru_b_a[0])
    cols[:, C_BX:C_BX + 16] = pc(lru_b_x[0])
    cols[:, C_LAM:C_LAM + 16] = pc(lru_lambda[0])
    cols[0:NH, C_FB] = np.asarray(forget_b[0], np.float32)
    ident = np.eye(128, dtype=np.float32)
    tri = np.triu(np.ones((128, 128), np.float32))
    return cols, ident, tri


_NC_CACHE = {}


def make_in_maps(inputs, batches):
    f = lambda a: np.ascontiguousarray(np.asarray(a, np.float32))
    cols, ident, tri = host_consts(inputs["norm_g"], inputs["conv_w"], inputs["conv_b"], inputs["lru_b_a"],
                                   inputs["lru_b_x"], inputs["lru_lambda"], inputs["forget_b"])
    shared = {
        "meta": f(inputs["meta_tokens"]), "cols": cols, "ident": ident, "tri": tri,
        "w_gu1": f(inputs["ffn1_w_gu"][0]), "w_dn1": f(inputs["ffn1_w_down"][0]), "w_in": f(inputs["w_in"][0]),
        "lru_wa": f(inputs["lru_w_a"][0]), "lru_wx": f(inputs["lru_w_x"][0]),
        "w_ro": f(inputs["w_rnn_out"][0]), "w_ao": f(inputs["w_attn_out"][0]), "w_o": f(inputs["w_o"][0]),
        "w_gu2": f(inputs["ffn2_w_gu"][0]), "w_dn2": f(inputs["ffn2_w_down"][0]),
    }
    xs = np.asarray(inputs["x"], np.float32)
    return [dict(shared, x=np.ascontiguousarray(xs[b])) for b in batches]


def kernel(**inputs):
    if "nc" not in _NC_CACHE:
        _NC_CACHE["nc"] = build()
    nc = _NC_CACHE["nc"]
    in_maps = make_in_maps(inputs, range(8))
    res = run_bass_kernel_spmd(nc, in_maps, core_ids=list(range(8)))
    return np.stack([np.asarray(r["out"], np.float32) for r in res.results], axis=0)
```

```python
import numpy as np
from contextlib import ExitStack
import concourse.bass as bass
import concourse.mybir as mybir
from concourse.bass_utils import run_bass_kernel_spmd

F32 = mybir.dt.float32
BF16 = mybir.dt.bfloat16
U8 = mybir.dt.uint8
AF = mybir.ActivationFunctionType
ALU = mybir.AluOpType

D = 2048
NCH = 16
SEQ = 2048
NMETA = 16
T = SEQ + NMETA
DFF = 5632
NFC = 44
NH = 16
NIN = 14352
EPS = 1e-6
NPASS = 4
PW = T // NPASS
SEGW = PW // 2
SCALE = 128 ** -0.5
O_XR, O_GR, O_Q, O_K, O_V, O_FL, O_GRNN, O_GATT = 0, 2048, 4096, 6144, 8192, 10240, 10256, 12304
C_G, C_CW, C_CB, C_BA, C_BX, C_LAM, C_FB, NCOLS = 0, 96, 160, 176, 192, 208, 224, 256

COMPUTE = ("pe", "act", "dve", "pool")


class Res:
    __slots__ = ("name", "w", "rs", "rd")

    def __init__(self, name):
        self.name = name
        self.w = None
        self.rs = {}
        self.rd = []


class Op:
    __slots__ = ("eng", "fn", "deps", "idx", "need_inc", "semval", "is_dma", "dsem", "dval", "raw_same")

    def __init__(self, eng, fn, is_dma):
        self.eng = eng
        self.fn = fn
        self.deps = []
        self.idx = -1
        self.need_inc = False
        self.semval = 0
        self.is_dma = is_dma
        self.dsem = None
        self.dval = 0
        self.raw_same = []


class Sched:
    def __init__(self, K=8):
        self.ops = {e: [] for e in ("pe", "act", "dve", "pool", "sp")}
        self.K = K
        self.all_ops = []
        self.fence = {e: [] for e in self.ops}
        self.live_dma = []

    def _add(self, eng, fn, reads, writes, is_dma):
        op = Op(eng, fn, is_dma)
        cand = []
        for r in reads:
            if r.w is not None:
                cand.append((r.w, True))
        for w in writes:
            if w.w is not None:
                cand.append((w.w, False))
            for rd in w.rs.values():
                cand.append((rd, False))
            for rd in w.rd:
                cand.append((rd, False))
        for d in self.fence[eng]:
            cand.append((d, False))
        self.fence[eng] = []
        best = {}
        seen = set()
        for d, raw in cand:
            if d is op:
                continue
            if d.is_dma or is_dma:
                if id(d) not in seen:
                    seen.add(id(d))
                    if not (d.eng == eng and not d.is_dma):
                        op.deps.append(d)
                continue
            if d.eng == eng:
                if raw and eng != "pe":
                    if id(d) not in seen:
                        seen.add(id(d))
                        op.raw_same.append(d)
                continue
            b = best.get(d.eng)
            if b is None or d.idx > b.idx:
                best[d.eng] = d
        op.deps.extend(best.values())
        for r in reads:
            if is_dma:
                r.rd.append(op)
            else:
                r.rs[eng] = op
        for w in writes:
            w.w = op
            w.rs = {}
            w.rd = []
        op.idx = len(self.ops[eng])
        self.ops[eng].append(op)
        self.all_ops.append(op)
        if is_dma:
            self.live_dma.append(op)
        return op

    def op(self, eng, fn, reads=(), writes=()):
        return self._add(eng, fn, reads, writes, False)

    def dma(self, eng, fn, reads=(), writes=()):
        return self._add(eng, fn, reads, writes, True)

    def barrier(self):
        lasts = []
        for e in COMPUTE:
            for op in reversed(self.ops[e]):
                if not op.is_dma:
                    lasts.append(op)
                    break
        lasts.extend(self.live_dma)
        self.live_dma = []
        for e in self.fence:
            self.fence[e] = list(lasts)

    def emit(self, block, sems, dsems):
        for e in ("sp", "act", "pool"):
            n = 0
            for op in self.ops[e]:
                if op.is_dma:
                    op.dsem = dsems[e][n % self.K]
                    op.dval = 16 * (n // self.K + 1)
                    n += 1
        for op in self.all_ops:
            for d in op.deps:
                if not d.is_dma:
                    d.need_inc = True
            for d in op.raw_same:
                if op.idx - d.idx <= 3:
                    d.need_inc = True
        for e in COMPUTE:
            cnt = 0
            for op in self.ops[e]:
                if op.is_dma:
                    continue
                if op.need_inc:
                    cnt += 1
                    op.semval = cnt
        engobj = {"pe": "tensor", "act": "scalar", "dve": "vector", "pool": "gpsimd", "sp": "sync"}

        def run(ename, eng):
            waited = {}

            def wait(sem, val):
                k = id(sem)
                if waited.get(k, 0) >= val:
                    return
                waited[k] = val
                eng.wait_ge(sem, val)

            finals = {}
            for op in self.ops[ename]:
                for d in op.deps:
                    if d.is_dma:
                        wait(d.dsem, d.dval)
                    else:
                        wait(sems[d.eng], d.semval)
                for d in op.raw_same:
                    if op.idx - d.idx <= 3:
                        wait(sems[ename], d.semval)
                if op.is_dma:
                    if op.dval > 16:
                        wait(op.dsem, op.dval - 16)
                    ins = op.fn(eng)
                    ins.then_inc(op.dsem, 16)
                    finals[id(op.dsem)] = (op.dsem, op.dval)
                else:
                    ins = op.fn(eng)
                    if op.need_inc:
                        ins.then_inc(sems[ename], 1)
            for sem, val in finals.values():
                wait(sem, val)

        for ename in ("sp", "pe", "act", "dve", "pool"):
            if self.ops[ename]:
                getattr(block, engobj[ename])(lambda eng, _e=ename: run(_e, eng))


class Arena:
    def __init__(self, ap, nbytes):
        self.ap = ap
        self.n = nbytes
        self.off = 0

    def mark(self):
        return self.off

    def release(self, m):
        self.off = m

    def alloc(self, dtype, shape):
        esz = 2 if dtype == BF16 else 4
        n = 1
        for s in shape:
            n *= s
        nb = (n * esz + 63) // 64 * 64
        assert self.off + nb <= self.n, f"SBUF arena overflow {self.off}+{nb}>{self.n}"
        v = self.ap[:, self.off:self.off + n * esz].bitcast(dtype)
        self.off += nb
        if len(shape) == 2:
            v = v.rearrange("p (a b) -> p a b", a=shape[0])
        elif len(shape) == 3:
            v = v.rearrange("p (a b c) -> p a b c", a=shape[0], b=shape[1])
        return v


class Ring:
    def __init__(self, A, S, name, dtype, shape, n):
        self.bufs = [(A.alloc(dtype, shape), Res(f"{name}{i}")) for i in range(n)]
        self.i = 0

    def next(self):
        b = self.bufs[self.i % len(self.bufs)]
        self.i += 1
        return b


def segs_of(c0, c1, w):
    out = []
    c = c0
    while c < c1:
        n = min(w, c1 - c)
        out.append((c, n))
        c += n
    return out


def build(debug=False, phases=("tin", "ffn1", "mix", "ffn2", "tout")):
    nc = bass.Bass("TRN2", target_bir_lowering=False)

    def din(name, shape, dt=F32):
        return nc.dram_tensor(name, list(shape), dt, kind="ExternalInput").ap()

    def dscr(name, shape, dt=F32):
        kind = "ExternalOutput" if debug else "Internal"
        return nc.dram_tensor(name, list(shape), dt, kind=kind).ap()

    x = din("x", [SEQ, D])
    meta = din("meta", [NMETA, D])
    cols_d = din("cols", [128, NCOLS])
    ident_d = din("ident", [128, 128])
    tri_d = din("tri", [128, 128])
    w_gu1 = din("w_gu1", [D, 2 * DFF])
    w_dn1 = din("w_dn1", [DFF, D])
    w_in = din("w_in", [D, NIN])
    lru_wa = din("lru_wa", [NCH, 128, 128])
    lru_wx = din("lru_wx", [NCH, 128, 128])
    w_ro = din("w_ro", [D, D])
    w_ao = din("w_ao", [D, D])
    w_o = din("w_o", [D, D])
    w_gu2 = din("w_gu2", [D, 2 * DFF])
    w_dn2 = din("w_dn2", [DFF, D])
    out = nc.dram_tensor("out", [SEQ, D], F32, kind="ExternalOutput").ap()

    hT0 = dscr("hT0", [D, T])
    hT1 = dscr("hT1", [D, T])
    hT2 = dscr("hT2", [D, T])
    hT3 = dscr("hT3", [D, T])
    u2s = dscr("u2s", [D, T], BF16)
    yas = dscr("yas", [D, T], BF16)
    yrs = dscr("yrs", [D, T], BF16)
    yscr = dscr("yscr", [D, T])

    ARENA_BYTES = 207 * 1024
    es = ExitStack()
    with es:
        arena_t = es.enter_context(nc.sbuf_tensor("arena", [128, ARENA_BYTES], U8))
        ps_t = es.enter_context(nc.psum_tensor("ps", [128, 8 * 512], F32))
        sems = {e: es.enter_context(nc.semaphore("s_" + e)) for e in COMPUTE}
        dsems = {e: [es.enter_context(nc.semaphore(f"d_{e}{i}")) for i in range(8)] for e in ("sp", "act", "pool")}
        block = es.enter_context(nc.Block())
        S = Sched()
        A = Arena(arena_t, ARENA_BYTES)
        bank = [ps_t[:, b * 512:(b + 1) * 512] for b in range(8)]
        rbank = [Res(f"bank{b}") for b in range(8)]

        cols = A.alloc(F32, [NCOLS])
        dcol = A.alloc(F32, [160])
        ident = A.alloc(F32, [128])
        identb = A.alloc(BF16, [128])
        trib = A.alloc(BF16, [128])
        trif = A.alloc(F32, [128])
        onesb = A.alloc(BF16, [128])
        ones16 = A.alloc(F32, [128])
        r_const = Res("const")
        S.dma("sp", lambda e: e.dma_start(out=cols, in_=cols_d), writes=[r_const])
        S.dma("sp", lambda e: e.dma_start(out=ident, in_=ident_d), writes=[r_const])
        S.dma("sp", lambda e: e.dma_start(out=trif, in_=tri_d), writes=[r_const])
        S.op("dve", lambda e: e.tensor_copy(out=trib, in_=trif), reads=[r_const], writes=[r_const])
        S.op("dve", lambda e: e.tensor_copy(out=identb, in_=ident), reads=[r_const], writes=[r_const])
        S.op("dve", lambda e: e.memset(onesb, 1.0), writes=[r_const])
        S.op("dve", lambda e: e.memset(ones16, 1.0), writes=[r_const])
        sqrtD = float(np.sqrt(D))
        for n in range(6):
            f = sqrtD * (0.5 if n in (1, 5) else 1.0)
            S.op("dve", lambda e, n=n, f=f: e.tensor_scalar(out=dcol[:, n * 16:(n + 1) * 16], in0=cols[:, C_G + n * 16:C_G + (n + 1) * 16],
                                                             scalar1=f, scalar2=None, op0=ALU.mult), reads=[r_const], writes=[r_const])
        S.op("act", lambda e: e.activation(out=dcol[:, 96:112], in_=cols[:, C_LAM:C_LAM + 16], func=AF.Sigmoid), reads=[r_const], writes=[r_const])
        S.op("act", lambda e: e.activation(out=dcol[:, 96:112], in_=dcol[:, 96:112], func=AF.Ln), reads=[r_const], writes=[r_const])
        S.op("dve", lambda e: e.tensor_scalar(out=dcol[:, 112:128], in0=dcol[:, 96:112], scalar1=16.0, scalar2=None, op0=ALU.mult), reads=[r_const], writes=[r_const])
        S.op("dve", lambda e: e.tensor_scalar(out=dcol[:, 96:112], in0=dcol[:, 96:112], scalar1=8.0, scalar2=None, op0=ALU.mult), reads=[r_const], writes=[r_const])
        base_mark = A.mark()

        def gs_col(n, c):
            return dcol[:, n * 16 + c:n * 16 + c + 1]

        pool_rr = [0]

        def load_slab(dst, rdst, W, r0, nk, c0, ncols):
            S.dma("pool", lambda e: e.dma_start(out=dst, in_=W[r0:r0 + nk * 128, c0:c0 + ncols].rearrange("(a p) c -> p a c", p=128)),
                  writes=[rdst])

        def rstd_from(ps_ap, n, dst, rdst, rps):
            S.op("act", lambda e: e.activation(out=dst, in_=ps_ap, func=AF.Sqrt, bias=float(D * EPS)), reads=[rps], writes=[rdst])
            S.op("dve", lambda e: e.reciprocal(out=dst, in_=dst), reads=[rdst], writes=[rdst])

        def pos_tiles():
            return segs_of(0, T, 128)

        if "tin" in phases:
            m = A.mark()
            xt_ring = Ring(A, S, "xt", F32, [D], 3)
            st_ring = Ring(A, S, "st", F32, [NCH, 128], 3)
            def tin_load(ti):
                p0, n = pos_tiles()[ti]
                xt, rxt = xt_ring.next()
                if ti == 0:
                    S.dma("sp", lambda e: e.dma_start(out=xt[0:NMETA, :], in_=meta), writes=[rxt])
                    S.dma("sp", lambda e: e.dma_start(out=xt[NMETA:128, :], in_=x[0:128 - NMETA, :]), writes=[rxt])
                else:
                    S.dma("sp", lambda e: e.dma_start(out=xt[0:n, :], in_=x[p0 - NMETA:p0 - NMETA + n, :]), writes=[rxt])
                return xt, rxt

            nxt = tin_load(0)
            for ti, (p0, n) in enumerate(pos_tiles()):
                xt, rxt = nxt
                if ti + 1 < len(pos_tiles()):
                    nxt = tin_load(ti + 1)
                st, rst = st_ring.next()
                for g in range(4):
                    b = (ti * 4 + g) % 8
                    for j in range(4):
                        c = g * 4 + j
                        S.op("pe", lambda e, b=b, j=j, c=c, xt=xt, n=n: e.transpose(out=bank[b][:, j * 128:j * 128 + n], in_=xt[0:n, c * 128:(c + 1) * 128], identity=ident[0:n, 0:n]),
                             reads=[rxt, r_const], writes=[rbank[b]])
                    eng = "act" if g % 2 == 0 else "dve"
                    if eng == "act":
                        S.op("act", lambda e, b=b, g=g, st=st, n=n: e.activation(out=st[:, g * 4:(g + 1) * 4, 0:n], in_=bank[b].rearrange("p (a b) -> p a b", a=4)[:, :, 0:n], func=AF.Copy),
                             reads=[rbank[b]], writes=[rst])
                    else:
                        S.op("dve", lambda e, b=b, g=g, st=st, n=n: e.tensor_copy(out=st[:, g * 4:(g + 1) * 4, 0:n], in_=bank[b].rearrange("p (a b) -> p a b", a=4)[:, :, 0:n]),
                             reads=[rbank[b]], writes=[rst])
                S.dma("sp", lambda e, st=st, p0=p0, n=n: e.dma_start(out=hT0[:, p0:p0 + n].rearrange("(a p) c -> p a c", p=128), in_=st[:, :, 0:n]),
                      reads=[rst], writes=[])
            S.barrier()
            A.release(m)

        def sumsq(src_fn, rsrc, ncols_segs, sq_ring, banks):
            for c in range(NCH):
                sq, rsq = sq_ring.next()
                for si, (c0, n) in enumerate(ncols_segs):
                    S.op("act", lambda e, c=c, c0=c0, n=n, sq=sq, si=si: e.activation(out=sq[:, si, 0:n], in_=src_fn(c, c0, n), func=AF.Square),
                         reads=[rsrc[c // 4] if isinstance(rsrc, list) else rsrc], writes=[rsq])
                for si, (c0, n) in enumerate(ncols_segs):
                    b = banks[si]
                    S.op("pe", lambda e, c=c, n=n, sq=sq, si=si, b=b: e.matmul(bank[b][:, 0:n], lhsT=onesb, rhs=sq[:, si, 0:n], start=(c == 0), stop=(c == NCH - 1)),
                         reads=[rsq, r_const], writes=[rbank[b]])

        def ffn_phase(hin, hout, w_gu, w_dn, n_pre, n_post, final_out=False):
            NP3, PW3, SG = 3, 688, 344
            segs = [(0, SG), (SG, SG)]
            m = A.mark()
            uTs = [A.alloc(BF16, [NCH, PW3]) for _ in range(2)]
            r_uTs = [Res("uTa"), Res("uTb")]
            act = A.alloc(BF16, [NFC, PW3])
            r_act = Res("act")
            rstd_pre = A.alloc(F32, [PW3])
            rstd_post = A.alloc(F32, [PW3])
            r_rpre, r_rpost = Res("rpre"), Res("rpost")
            hc_ring = Ring(A, S, "hc", F32, [PW3], 5)
            yc_ring = Ring(A, S, "yc", F32, [PW3], 3)
            pn_ring = Ring(A, S, "pn", F32, [PW3], 3)
            sq_ring = Ring(A, S, "sq", BF16, [PW3], 4)
            sl_ring = Ring(A, S, "sl", F32, [PW3], 3)
            w_ring = Ring(A, S, "wff", BF16, [NFC, 128], 4)
            r_ys = [Res(f"ys{c}") for c in range(NCH)]
            ot_ring = Ring(A, S, "ot", F32, [6, 128], 2) if final_out else None
            evn = [0]

            def evac(out_ap, in_ap, rin, rout):
                evn[0] += 1
                if evn[0] % 2 == 0:
                    S.op("act", lambda e: e.activation(out=out_ap, in_=in_ap, func=AF.Copy), reads=[rin], writes=[rout])
                else:
                    S.op("dve", lambda e: e.tensor_copy(out=out_ap, in_=in_ap), reads=[rin], writes=[rout])

            w_res2 = {id(r_): Res("wff_b") for (_, r_) in w_ring.bufs}
            gb = [0]

            def gbank():
                b = gb[0] % 6
                gb[0] += 1
                return b

            def stats_mm(sq, rsq, c):
                for si, (c0, n) in enumerate(segs):
                    S.op("pe", lambda e, si=si, c0=c0, n=n: e.matmul(bank[6 + si][:, 0:n], lhsT=onesb, rhs=sq[:, c0:c0 + n], start=(c == 0), stop=(c == NCH - 1)),
                         reads=[rsq, r_const], writes=[rbank[6 + si]])

            def rstd_step(dst, rdst):
                for si, (c0, n) in enumerate(segs):
                    rstd_from(bank[6 + si][:, 0:n], n, dst[:, c0:c0 + n], rdst, rbank[6 + si])

            def prenorm_steps(p, depth=2):
                P0 = p * PW3
                uT, ruT = uTs[p % 2], r_uTs[p % 2]
                steps = []

                loaded = []
                order = list(range(NCH)) + list(range(NCH))

                def prefetch():
                    if len(loaded) < depth and order:
                        c_ = order.pop(0)
                        hc, rhc = hc_ring.next()
                        S.dma("sp", lambda e: e.dma_start(out=hc, in_=hin[c_ * 128:(c_ + 1) * 128, P0:P0 + PW3]), writes=[rhc])
                        loaded.append((hc, rhc))

                def st_stats(c):
                    for _ in range(depth):
                        prefetch()
                    hc, rhc = loaded.pop(0)
                    sq, rsq = sq_ring.next()
                    S.op("act", lambda e: e.activation(out=sq, in_=hc, func=AF.Square), reads=[rhc], writes=[rsq])
                    prefetch()
                    return lambda: stats_mm(sq, rsq, c)

                def st_u(c):
                    prefetch()
                    hc, rhc = loaded.pop(0)
                    S.op("dve", lambda e: e.scalar_tensor_tensor(out=uT[:, c, :], in0=hc, scalar=gs_col(n_pre, c), in1=rstd_pre, op0=ALU.mult, op1=ALU.mult),
                         reads=[rhc, r_rpre, r_const], writes=[ruT])
                    prefetch()

                for c in range(NCH):
                    steps.append(lambda c=c: st_stats(c))
                steps.append(lambda: rstd_step(rstd_pre, r_rpre))
                for c in range(NCH):
                    steps.append(lambda c=c: st_u(c))
                return steps

            def gateup(p, f):
                uT, ruT = uTs[p % 2], r_uTs[p % 2]
                wsl, rws = w_ring.next()
                rws2 = w_res2[id(rws)]
                load_slab(wsl[:, 0:NCH, :], rws, w_gu, 0, NCH, f * 128, 128)
                load_slab(wsl[:, NCH:2 * NCH, :], rws2, w_gu, 0, NCH, DFF + f * 128, 128)
                bs = [gbank() for _ in range(4)]
                for wi in range(2):
                    rw_ = rws if wi == 0 else rws2
                    for si, (c0, n) in enumerate(segs):
                        b = bs[wi * 2 + si]
                        for k in range(NCH):
                            S.op("pe", lambda e, b=b, n=n, wi=wi, k=k, c0=c0: e.matmul(bank[b][:, 0:n], lhsT=wsl[:, wi * NCH + k, :], rhs=uT[:, k, c0:c0 + n], start=(k == 0), stop=(k == NCH - 1)),
                                 reads=[rw_, ruT], writes=[rbank[b]])
                sl, rsl = sl_ring.next()
                for si, (c0, n) in enumerate(segs):
                    bg, bu = bs[si], bs[2 + si]
                    S.op("act", lambda e, bg=bg, n=n, c0=c0: e.activation(out=sl[:, c0:c0 + n], in_=bank[bg][:, 0:n], func=AF.Silu),
                         reads=[rbank[bg]], writes=[rsl])
                    S.op("dve", lambda e, bu=bu, n=n, c0=c0: e.tensor_tensor(out=act[:, f, c0:c0 + n], in0=bank[bu][:, 0:n], in1=sl[:, c0:c0 + n], op=ALU.mult),
                         reads=[rbank[bu], rsl], writes=[r_act])

            def down_chunk(p, dc, pending):
                P0 = p * PW3
                wd, rwd = w_ring.next()
                rwd2 = w_res2[id(rwd)]
                S.dma("pool", lambda e: e.dma_start(out=wd, in_=w_dn[:, dc * 128:(dc + 1) * 128].rearrange("(a p) c -> p a c", p=128)), writes=[rwd, rwd2])
                bs = [gbank(), gbank()]
                for si, (c0, n) in enumerate(segs):
                    b = bs[si]
                    for k in range(NFC):
                        S.op("pe", lambda e, b=b, n=n, k=k, c0=c0: e.matmul(bank[b][:, 0:n], lhsT=wd[:, k, :], rhs=act[:, k, c0:c0 + n], start=(k == 0), stop=(k == NFC - 1)),
                             reads=[rwd, rwd2, r_act], writes=[rbank[b]])
                while pending:
                    pending.pop(0)()
                yc, ryc = yc_ring.next()
                for si, (c0, n) in enumerate(segs):
                    b = bs[si]
                    S.op("act", lambda e, b=b, n=n, c0=c0: e.activation(out=yc[:, c0:c0 + n], in_=bank[b][:, 0:n], func=AF.Copy),
                         reads=[rbank[b]], writes=[ryc])
                sq, rsq = sq_ring.next()
                S.op("act", lambda e: e.activation(out=sq, in_=yc, func=AF.Square), reads=[ryc], writes=[rsq])
                pending.append(lambda: stats_mm(sq, rsq, dc))
                S.dma("sp", lambda e: e.dma_start(out=yscr[dc * 128:(dc + 1) * 128, P0:P0 + PW3], in_=yc), reads=[ryc], writes=[r_ys[dc]])

            def postnorm_steps(p):
                P0 = p * PW3
                steps = [lambda: rstd_step(rstd_post, r_rpost)]

                ld = {}

                def loads(c):
                    if c >= NCH or c in ld:
                        return
                    pn, rpn = pn_ring.next()
                    S.dma("sp", lambda e: e.dma_start(out=pn, in_=yscr[c * 128:(c + 1) * 128, P0:P0 + PW3]), reads=[r_ys[c]], writes=[rpn])
                    hc, rhc = hc_ring.next()
                    S.dma("sp", lambda e: e.dma_start(out=hc, in_=hin[c * 128:(c + 1) * 128, P0:P0 + PW3]), writes=[rhc])
                    ld[c] = (pn, rpn, hc, rhc)

                def st(c):
                    loads(c)
                    loads(c + 1)
                    if flushing[0]:
                        loads(c + 2)
                    pn, rpn, hc, rhc = ld.pop(c)
                    S.op("dve", lambda e: e.scalar_tensor_tensor(out=pn, in0=pn, scalar=gs_col(n_post, c), in1=rstd_post, op0=ALU.mult, op1=ALU.mult),
                         reads=[rpn, r_rpost, r_const], writes=[rpn])
                    S.op("dve", lambda e: e.tensor_tensor(out=pn, in0=pn, in1=hc, op=ALU.add), reads=[rpn, rhc], writes=[rpn])
                    if not final_out:
                        S.dma("sp", lambda e: e.dma_start(out=hout[c * 128:(c + 1) * 128, P0:P0 + PW3], in_=pn), reads=[rpn], writes=[])
                        return
                    return lambda: out_part(c, pn, rpn)

                def out_part(c, pn, rpn):
                    ot, rot = ot_ring.next()
                    blks = segs_of(0, PW3, 128)
                    for g0 in range(0, len(blks), 4):
                        b = gbank()
                        grp = blks[g0:g0 + 4]
                        for j, (q0_, n_) in enumerate(grp):
                            S.op("pe", lambda e, b=b, j=j, q0_=q0_, n_=n_: e.transpose(out=bank[b][0:n_, j * 128:(j + 1) * 128], in_=pn[:, q0_:q0_ + n_], identity=ident),
                                 reads=[rpn, r_const], writes=[rbank[b]])
                        nfull = sum(1 for (_, n_) in grp if n_ == 128)
                        if nfull:
                            evac(ot[:, g0:g0 + nfull, :], bank[b][:, 0:nfull * 128].rearrange("p (a b) -> p a b", b=128), rbank[b], rot)
                        for j, (q0_, n_) in enumerate(grp):
                            if n_ < 128:
                                evac(ot[0:n_, g0 + j, :], bank[b][0:n_, j * 128:(j + 1) * 128], rbank[b], rot)
                    bi = 0
                    if P0 == 0:
                        S.dma("sp", lambda e: e.dma_start(out=out[0:128 - NMETA, c * 128:(c + 1) * 128], in_=ot[NMETA:128, 0, :]), reads=[rot], writes=[])
                        bi = 1
                    nfb = sum(1 for (_, n_) in blks if n_ == 128)
                    r0 = P0 + bi * 128 - NMETA
                    S.dma("sp", lambda e: e.dma_start(out=out[r0:r0 + (nfb - bi) * 128, c * 128:(c + 1) * 128].rearrange("(b p) d -> p b d", p=128), in_=ot[:, bi:nfb, :]),
                          reads=[rot], writes=[])
                    for j, (q0_, n_) in enumerate(blks):
                        if n_ < 128:
                            r1 = P0 + q0_ - NMETA
                            S.dma("sp", lambda e, j=j, n_=n_, r1=r1: e.dma_start(out=out[r1:r1 + n_, c * 128:(c + 1) * 128], in_=ot[0:n_, j, :]), reads=[rot], writes=[])

                for c in range(NCH):
                    steps.append(lambda c=c: st(c))
                return steps

            late = []
            flushing = [False]

            def run_step(stp):
                r_ = stp()
                if r_ is not None:
                    late.append(r_)

            def run_late():
                while late:
                    late.pop(0)()

            for stp in prenorm_steps(0, depth=4):
                run_step(stp)
                run_late()
            post = []
            for p in range(NP3):
                pre = prenorm_steps(p + 1) if p + 1 < NP3 else []
                for f in range(NFC):
                    gateup(p, f)
                    run_late()
                    if post:
                        run_step(post.pop(0))
                    if f >= 6 and pre:
                        run_step(pre.pop(0))
                run_late()
                while pre:
                    run_step(pre.pop(0))
                    run_late()
                while post:
                    run_step(post.pop(0))
                    run_late()
                pending = []
                for dc in range(NCH):
                    down_chunk(p, dc, pending)
                while pending:
                    pending.pop(0)()
                post = postnorm_steps(p)
            flushing[0] = True
            while post:
                run_step(post.pop(0))
                run_late()
            S.barrier()
            A.release(m)

        def mix_phase():
            mtop = A.mark()
            segs2 = [(0, SEGW), (SEGW, SEGW)]
            u2 = A.alloc(BF16, [NCH, T])
            r_u2 = Res("u2")
            Fcol = A.alloc(F32, [17, 16])
            NQS = 9
            FrefB = A.alloc(F32, [16, NQS])
            r_F = Res("Fstuff")
            pj = [0]

            def pbank():
                b = 6 + (pj[0] % 2)
                pj[0] += 1
                return b

            ev = [0]

            def evac(out_ap, in_ap, rin, rout):
                ev[0] += 1
                if ev[0] % 2 == 0:
                    S.op("act", lambda e: e.activation(out=out_ap, in_=in_ap, func=AF.Copy), reads=[rin], writes=[rout])
                else:
                    S.op("dve", lambda e: e.tensor_copy(out=out_ap, in_=in_ap), reads=[rin], writes=[rout])

            m0 = A.mark()
            hT_ring = Ring(A, S, "hTm", F32, [NCH, PW], 2)
            rstd = A.alloc(F32, [2, SEGW])
            r_rstd = Res("rstdm")
            sq_ring = Ring(A, S, "sqm", BF16, [2, SEGW], 4)

            m0_res = {}

            def m0_load(p):
                hT_, r_hT_ = hT_ring.next()
                rl = m0_res.setdefault(id(r_hT_), [Res("hTm_a"), Res("hTm_b"), Res("hTm_c"), Res("hTm_d")])
                for g in range(4):
                    S.dma("sp", lambda e, g=g: e.dma_start(out=hT_[:, 4 * g:4 * g + 4, :], in_=hT1[g * 512:(g + 1) * 512, p * PW:(p + 1) * PW].rearrange("(a p) c -> p a c", p=128)),
                          writes=[rl[g]])
                return hT_, rl

            nxt0 = m0_load(0)
            for p in range(NPASS):
                P0 = p * PW
                hT, r_hT = nxt0
                if p + 1 < NPASS:
                    nxt0 = m0_load(p + 1)
                sumsq(lambda c, c0, n, hT=hT: hT[:, c, c0:c0 + n], r_hT, segs2, sq_ring, [4, 5])
                r_hT_l = r_hT
                for si, (c0, n) in enumerate(segs2):
                    rstd_from(bank[4 + si][:, 0:n], n, rstd[:, si, 0:n], r_rstd, rbank[4 + si])
                for c in range(NCH):
                    for si, (c0, n) in enumerate(segs2):
                        S.op("dve", lambda e, c=c, c0=c0, n=n, si=si, P0=P0, hT=hT: e.scalar_tensor_tensor(out=u2[:, c, P0 + c0:P0 + c0 + n], in0=hT[:, c, c0:c0 + n], scalar=gs_col(2, c),
                                                                                                   in1=rstd[:, si, 0:n], op0=ALU.mult, op1=ALU.mult),
                             reads=[r_hT_l[c // 4], r_rstd, r_const], writes=[r_u2])
            S.barrier()
            A.release(m0)

            mF = A.mark()
            wfl = A.alloc(BF16, [NCH, 16])
            r_wfl = Res("wfl")
            load_slab(wfl, r_wfl, w_in, 0, NCH, O_FL, 16)
            Ff = A.alloc(F32, [T])
            lf = A.alloc(F32, [T])
            onesT = A.alloc(F32, [T])
            Fend = A.alloc(F32, [NQS])
            rhsBD = A.alloc(F32, [16, NQS])
            r_Ff, r_lf, r_on = Res("Ff"), Res("lf"), Res("onesT")
            def F1():
                S.op("dve", lambda e: e.memset(onesT[0:16, :], 1.0), writes=[r_on])
                for (c0, n) in segs_of(0, T, 512):
                    b = pbank()
                    for k in range(NCH):
                        S.op("pe", lambda e, b=b, n=n, k=k, c0=c0: e.matmul(bank[b][0:16, 0:n], lhsT=wfl[:, k, :], rhs=u2[:, k, c0:c0 + n], start=(k == 0), stop=(k == NCH - 1)),
                             reads=[r_wfl, r_u2], writes=[rbank[b]])
                    S.op("act", lambda e, b=b, n=n, c0=c0: e.activation(out=lf[0:16, c0:c0 + n], in_=bank[b][0:16, 0:n], func=AF.Sigmoid, bias=cols[0:16, C_FB:C_FB + 1]),
                         reads=[rbank[b], r_const], writes=[r_lf])
                S.op("act", lambda e: e.activation(out=lf[0:16, :], in_=lf[0:16, :], func=AF.Ln), reads=[r_lf], writes=[r_lf])
                S.op("dve", lambda e: e.tensor_tensor_scan(out=Ff[0:16, :], data0=onesT[0:16, :], data1=lf[0:16, :], initial=0.0, op0=ALU.mult, op1=ALU.add),
                     reads=[r_on, r_lf], writes=[r_Ff])

            def F2():
                for i, (p0, n) in enumerate(pos_tiles()):
                    b = pbank()
                    S.op("pe", lambda e, b=b, p0=p0, n=n: e.matmul(bank[b][0:n, 0:16], lhsT=Ff[0:16, p0:p0 + n], rhs=ident[0:16, 0:16], start=True, stop=True),
                         reads=[r_Ff, r_const], writes=[rbank[b]])
                    S.op("dve", lambda e, b=b, i=i, n=n: e.tensor_copy(out=Fcol[0:n, i, :], in_=bank[b][0:n, 0:16]), reads=[rbank[b]], writes=[r_F])
                S.op("dve", lambda e: e.tensor_copy(out=Fend[0:16, 0:8], in_=Ff[0:16, 0:2048].rearrange("p (a b) -> p a b", b=256)[:, :, 255]), reads=[r_Ff], writes=[r_lf])
                S.op("dve", lambda e: e.tensor_copy(out=Fend[0:16, 8:9], in_=Ff[0:16, T - 1:T]), reads=[r_Ff], writes=[r_lf])
                for h in range(NH):
                    S.op("dve", lambda e, h=h: e.tensor_scalar(out=rhsBD[0:16, h, :], in0=Fend[0:16, :], scalar1=ident[0:16, h:h + 1], scalar2=None, op0=ALU.mult),
                         reads=[r_lf, r_const], writes=[r_on])
                b = pbank()
                S.op("pe", lambda e, b=b: e.matmul(bank[b][:, 0:16 * NQS], lhsT=ones16[0:16, :], rhs=rhsBD[0:16].rearrange("p a b -> p (a b)"), start=True, stop=True),
                     reads=[r_on, r_const], writes=[rbank[b]])
                S.op("dve", lambda e, b=b: e.tensor_copy(out=FrefB.rearrange("p a b -> p (a b)"), in_=bank[b][:, 0:16 * NQS]), reads=[rbank[b]], writes=[r_F])


            mA = A.mark()
            wqk_ring = Ring(A, S, "wqk", BF16, [NCH, 128], 4)
            wv_ring = Ring(A, S, "wv", BF16, [NCH, 512], 1)
            qk_ring = Ring(A, S, "qk", BF16, [T], 4)
            V_ring = Ring(A, S, "V", BF16, [17, 512], 2)
            PT_ring = Ring(A, S, "PT", BF16, [512], 4)
            bc_ring = Ring(A, S, "bc", F32, [17, NQS], 2)
            dn_ring = Ring(A, S, "dn", F32, [512], 2)
            ya_ring = Ring(A, S, "ya", BF16, [T], 2)
            ktiles = pos_tiles()
            qblocks = segs_of(0, T, 512)
            sb = [0]

            def attention(h, hl, qT, rq, kT, rk, V, rV, ya, rya):
                tasks = []
                for j, (q0, qw) in enumerate(qblocks):
                    tl = [i for i, (k0, kn) in enumerate(ktiles) if k0 < q0 + qw]
                    for idx, i in enumerate(tl):
                        tasks.append((j, q0, qw, i, idx == 0, idx == len(tl) - 1))
                st = {}
                bc, rbc = bc_ring.next()
                for i, (k0, kn) in enumerate(ktiles):
                    S.op("dve", lambda e, i=i, kn=kn: e.tensor_scalar(out=bc[0:kn, i, :], in0=FrefB[0:kn, h, :], scalar1=Fcol[0:kn, i, h:h + 1], scalar2=None, op0=ALU.subtract),
                         reads=[r_F], writes=[rbc])

                def s1(t):
                    j, q0, qw, i, first, last = tasks[t]
                    k0, kn = ktiles[i]
                    qc0 = max(0, k0 - q0)
                    N = qw - qc0
                    diag = k0 >= q0
                    bs_ = sb[0] % 2
                    sb[0] += 1
                    S.op("pe", lambda e: e.matmul(bank[bs_][0:kn, 0:N], lhsT=kT[:, k0:k0 + kn], rhs=qT[:, q0 + qc0:q0 + qw], start=True, stop=True),
                         reads=[rq, rk], writes=[rbank[bs_]])
                    PT, rPT = PT_ring.next()
                    pa = q0 + qc0
                    while pa < q0 + qw:
                        pb = min((pa // 256 + 1) * 256, q0 + qw)
                        s0, sn, qs = pa - q0 - qc0, pb - pa, pa // 256
                        S.op("act", lambda e, s0=s0, sn=sn, qs=qs: e.activation(out=PT[0:kn, s0:s0 + sn], in_=bank[bs_][0:kn, s0:s0 + sn], func=AF.Exp,
                                                                              bias=bc[0:kn, i, qs:qs + 1], scale=float(SCALE)),
                             reads=[rbank[bs_], rbc], writes=[rPT])
                        pa = pb
                    if diag:
                        dn = min(128, N)
                        S.op("dve", lambda e: e.tensor_tensor(out=PT[0:kn, 0:dn], in0=PT[0:kn, 0:dn], in1=trib[0:kn, 0:dn], op=ALU.mult),
                             reads=[rPT, r_const], writes=[rPT])
                    st[t] = (PT, rPT, qc0, N)

                def s2(t):
                    j, q0, qw, i, first, last = tasks[t]
                    k0, kn = ktiles[i]
                    PT, rPT, qc0, N = st.pop(t)
                    bo, bd = 2 + (j % 2), 4 + (j % 2)
                    S.op("pe", lambda e: e.matmul(bank[bo][:, qc0:qw], lhsT=V[0:kn, i, hl * 128:(hl + 1) * 128], rhs=PT[0:kn, 0:N], start=first, stop=last),
                         reads=[rV, rPT], writes=[rbank[bo]])
                    S.op("pe", lambda e: e.matmul(bank[bd][:, qc0:qw], lhsT=onesb[0:kn, :], rhs=PT[0:kn, 0:N], start=first, stop=last),
                         reads=[r_const, rPT], writes=[rbank[bd]])
                    if last:
                        rd, rrd = dn_ring.next()
                        S.op("dve", lambda e: e.reciprocal(out=rd[:, 0:qw], in_=bank[bd][:, 0:qw]), reads=[rbank[bd]], writes=[rrd])
                        S.op("dve", lambda e: e.tensor_tensor(out=ya[:, q0:q0 + qw], in0=bank[bo][:, 0:qw], in1=rd[:, 0:qw], op=ALU.mult),
                             reads=[rbank[bo], rrd], writes=[rya])

                steps = [lambda: s1(0)]

                def mk(t):
                    def f_():
                        if t + 1 < len(tasks):
                            s1(t + 1)
                        s2(t)
                    return f_
                for t in range(len(tasks)):
                    steps.append(mk(t))
                return steps

            Vcur = {}

            def proj_groups(h):
                groups = []
                hg, hl = divmod(h, 4)
                if hl == 0:
                    wv, rwv = wv_ring.next()
                    V, rV = V_ring.next()
                    Vcur[hg] = (V, rV)

                    def ldv():
                        load_slab(wv, rwv, w_in, 0, NCH, O_V + hg * 512, 512)
                    groups.append(ldv)
                    for i, (p0, n) in enumerate(ktiles):
                        def gv(i=i, p0=p0, n=n):
                            b = pbank()
                            for k in range(NCH):
                                S.op("pe", lambda e, b=b, k=k: e.matmul(bank[b][0:n, :], lhsT=u2[:, k, p0:p0 + n], rhs=wv[:, k, :], start=(k == 0), stop=(k == NCH - 1)),
                                     reads=[rwv, r_u2], writes=[rbank[b]])
                            evac(V[0:n, i, :], bank[b][0:n, :], rbank[b], rV)
                        groups.append(gv)
                wq, rwq = wqk_ring.next()
                wk, rwk = wqk_ring.next()
                qT, rq = qk_ring.next()
                kT, rk = qk_ring.next()

                def ldqk():
                    load_slab(wq, rwq, w_in, 0, NCH, O_Q + h * 128, 128)
                    load_slab(wk, rwk, w_in, 0, NCH, O_K + h * 128, 128)
                groups.append(ldqk)
                for (wt, rwt, dst, rdst) in ((wq, rwq, qT, rq), (wk, rwk, kT, rk)):
                    for (c0, n) in segs_of(0, T, 344):
                        def gq(wt=wt, rwt=rwt, dst=dst, rdst=rdst, c0=c0, n=n):
                            b = pbank()
                            for k in range(NCH):
                                S.op("pe", lambda e, b=b, k=k: e.matmul(bank[b][:, 0:n], lhsT=wt[:, k, :], rhs=u2[:, k, c0:c0 + n], start=(k == 0), stop=(k == NCH - 1)),
                                     reads=[rwt, r_u2], writes=[rbank[b]])
                            evac(dst[:, c0:c0 + n], bank[b][:, 0:n], rbank[b], rdst)
                        groups.append(gq)
                return groups, (qT, rq, kT, rk)

            g0, qk_next = proj_groups(0)
            F1()
            for gi_, g_ in enumerate(g0):
                if gi_ == 10:
                    F2()
                g_()
            for h in range(NH):
                hg, hl = divmod(h, 4)
                qT, rq, kT, rk = qk_next
                V, rV = Vcur[hg]
                ya, rya = ya_ring.next()
                asteps = attention(h, hl, qT, rq, kT, rk, V, rV, ya, rya)
                if h + 1 < NH:
                    pg, qk_next = proj_groups(h + 1)
                else:
                    pg = []
                na, npg = len(asteps), len(pg)
                ai = 0
                for gi_, g_ in enumerate(pg):
                    g_()
                    tgt = (gi_ + 1) * na // max(npg, 1)
                    while ai < tgt:
                        asteps[ai]()
                        ai += 1
                while ai < na:
                    asteps[ai]()
                    ai += 1
                S.dma("sp", lambda e, h=h, ya=ya: e.dma_start(out=yas[h * 128:(h + 1) * 128, :], in_=ya), reads=[rya], writes=[])
                if h < 4:
                    S.dma("sp", lambda e, h=h: e.dma_start(out=u2s[h * 512:(h + 1) * 512, :].rearrange("(a p) c -> p a c", p=128), in_=u2[:, 4 * h:4 * h + 4, :]),
                          reads=[r_u2], writes=[])
            S.barrier()
            A.release(mF)

            mR = A.mark()
            lwa = A.alloc(BF16, [NCH, 128])
            lwx = A.alloc(BF16, [NCH, 128])
            r_lw = Res("lw")
            S.dma("pool", lambda e: e.dma_start(out=lwa, in_=lru_wa.rearrange("n i j -> i n j")), writes=[r_lw])
            S.dma("pool", lambda e: e.dma_start(out=lwx, in_=lru_wx.rearrange("n i j -> i n j")), writes=[r_lw])
            wr_ring = Ring(A, S, "wr", BF16, [NCH, 128], 6)
            xr_ring = Ring(A, S, "xr", F32, [T + 3], 2)
            xc_ring = Ring(A, S, "xc", F32, [T], 1)
            xcb_ring = Ring(A, S, "xcb", BF16, [T], 1)
            rr_ring = Ring(A, S, "rr", F32, [T], 1)
            ii_ring = Ring(A, S, "ii", F32, [T], 1)
            a2_ring = Ring(A, S, "a2", F32, [T], 1)
            gg_ring = Ring(A, S, "gg", BF16, [T], 2)
            hr_ring = Ring(A, S, "hr", F32, [T], 1)
            yo_ring = Ring(A, S, "yo", BF16, [T], 2)
            segs5 = segs_of(0, T, 344)

            def pcol(base, c):
                return cols[:, base + c:base + c + 1]

            pjb = [0]
            gtb = [0]

            def pbank4():
                b = pjb[0] % 4
                pjb[0] += 1
                return b

            def gbank4():
                b = 4 + gtb[0] % 4
                gtb[0] += 1
                return b

            stt = {}

            def stageA(c):
                wxr, rwxr = wr_ring.next()
                load_slab(wxr, rwxr, w_in, 0, NCH, O_XR + c * 128, 128)
                wgr, rwgr = wr_ring.next()
                load_slab(wgr, rwgr, w_in, 0, NCH, O_GR + c * 128, 128)
                xr, rxr = xr_ring.next()
                S.op("dve", lambda e: e.memset(xr[:, 0:3], 0.0), writes=[rxr])
                for (c0, n) in segs5:
                    b = pbank4()
                    for k in range(NCH):
                        S.op("pe", lambda e, b=b, n=n, k=k, c0=c0: e.matmul(bank[b][:, 0:n], lhsT=wxr[:, k, :], rhs=u2[:, k, c0:c0 + n], start=(k == 0), stop=(k == NCH - 1)),
                             reads=[rwxr, r_u2], writes=[rbank[b]])
                    evac(xr[:, 3 + c0:3 + c0 + n], bank[b][:, 0:n], rbank[b], rxr)
                stt[c] = dict(wgr=wgr, rwgr=rwgr, xr=xr, rxr=rxr)

            def stageB(c):
                d_ = stt[c]
                xr, rxr = d_["xr"], d_["rxr"]
                xc, rxc = xc_ring.next()
                S.op("dve", lambda e: e.tensor_scalar(out=xc, in0=xr[:, 0:T], scalar1=pcol(C_CW, c), scalar2=pcol(C_CB, c), op0=ALU.mult, op1=ALU.add),
                     reads=[rxr, r_const], writes=[rxc])
                for kk in range(1, 4):
                    S.op("dve", lambda e, kk=kk: e.scalar_tensor_tensor(out=xc, in0=xr[:, kk:kk + T], scalar=pcol(C_CW + kk * 16, c), in1=xc, op0=ALU.mult, op1=ALU.add),
                         reads=[rxr, rxc, r_const], writes=[rxc])
                xcb, rxcb = xcb_ring.next()
                S.op("act", lambda e: e.activation(out=xcb, in_=xc, func=AF.Copy), reads=[rxc], writes=[rxcb])
                d_.update(xc=xc, rxc=rxc, xcb=xcb, rxcb=rxcb)

            def stageC(c):
                d_ = stt[c]
                xcb, rxcb = d_["xcb"], d_["rxcb"]
                rr, rrr = rr_ring.next()
                ii, rii = ii_ring.next()
                for (lw, dst, rdst, bcol) in ((lwa, rr, rrr, C_BA), (lwx, ii, rii, C_BX)):
                    for (c0, n) in segs5:
                        b = gbank4()
                        S.op("pe", lambda e, b=b, n=n, c0=c0, lw=lw: e.matmul(bank[b][:, 0:n], lhsT=lw[:, c, :], rhs=xcb[:, c0:c0 + n], start=True, stop=True),
                             reads=[r_lw, rxcb], writes=[rbank[b]])
                        S.op("act", lambda e, b=b, n=n, c0=c0, dst=dst, bcol=bcol: e.activation(out=dst[:, c0:c0 + n], in_=bank[b][:, 0:n], func=AF.Sigmoid, bias=pcol(bcol, c)),
                             reads=[rbank[b], r_const], writes=[rdst])
                d_.update(rr=rr, rrr=rrr, ii=ii, rii=rii)

            def stageD(c):
                d_ = stt[c]
                rr, rrr, ii, rii, xc, rxc = d_["rr"], d_["rrr"], d_["ii"], d_["rii"], d_["xc"], d_["rxc"]
                a2, ra2 = a2_ring.next()
                S.op("act", lambda e: e.activation(out=a2, in_=rr, func=AF.Exp, scale=dcol[:, 112 + c:113 + c]), reads=[rrr, r_const], writes=[ra2])
                S.op("act", lambda e: e.activation(out=rr, in_=rr, func=AF.Exp, scale=dcol[:, 96 + c:97 + c]), reads=[rrr, r_const], writes=[rrr])
                S.op("dve", lambda e: e.tensor_scalar(out=a2, in0=a2, scalar1=1.0, scalar2=-1.0, op0=ALU.min, op1=ALU.mult), reads=[ra2], writes=[ra2])
                S.op("act", lambda e: e.activation(out=a2, in_=a2, func=AF.Sqrt, bias=1.0), reads=[ra2], writes=[ra2])
                S.op("dve", lambda e: e.tensor_tensor(out=ii, in0=ii, in1=xc, op=ALU.mult), reads=[rii, rxc], writes=[rii])
                S.op("dve", lambda e: e.tensor_tensor(out=ii, in0=ii, in1=a2, op=ALU.mult), reads=[rii, ra2], writes=[rii])
                hr, rhr = hr_ring.next()
                S.op("dve", lambda e: e.tensor_tensor_scan(out=hr, data0=rr, data1=ii, initial=0.0, op0=ALU.mult, op1=ALU.add),
                     reads=[rrr, rii], writes=[rhr])
                d_.update(hr=hr, rhr=rhr)

            def stageE(c):
                d_ = stt.pop(c)
                wgr, rwgr, hr, rhr = d_["wgr"], d_["rwgr"], d_["hr"], d_["rhr"]
                gg, rgg = gg_ring.next()
                for (c0, n) in segs5:
                    b = pbank4()
                    for k in range(NCH):
                        S.op("pe", lambda e, b=b, n=n, k=k, c0=c0: e.matmul(bank[b][:, 0:n], lhsT=wgr[:, k, :], rhs=u2[:, k, c0:c0 + n], start=(k == 0), stop=(k == NCH - 1)),
                             reads=[rwgr, r_u2], writes=[rbank[b]])
                    S.op("act", lambda e, b=b, n=n, c0=c0: e.activation(out=gg[:, c0:c0 + n], in_=bank[b][:, 0:n], func=AF.Gelu_apprx_tanh),
                         reads=[rbank[b]], writes=[rgg])
                yo, ryo = yo_ring.next()
                S.op("dve", lambda e: e.tensor_tensor(out=yo, in0=hr, in1=gg, op=ALU.mult), reads=[rhr, rgg], writes=[ryo])
                S.dma("sp", lambda e: e.dma_start(out=yrs[c * 128:(c + 1) * 128, :], in_=yo), reads=[ryo], writes=[])

            stageA(0)
            for c in range(NCH):
                stageB(c)
                if c + 1 < NCH:
                    stageA(c + 1)
                stageC(c)
                stageD(c)
                stageE(c)
            S.barrier()
            A.release(mtop)

            m3 = A.mark()
            NP3, PW3, SG = 3, 688, 344
            segs3 = [(0, SG), (SG, SG)]
            u2t = A.alloc(BF16, [NCH, PW3])
            yrt = A.alloc(BF16, [NCH, PW3])
            yat = A.alloc(BF16, [NCH, PW3])
            mg = A.alloc(BF16, [NCH, PW3])
            rstd3 = A.alloc(F32, [PW3])
            r_u2t, r_yrt, r_yat, r_mg, r_rstd3 = Res("u2t"), Res("yrt"), Res("yat"), Res("mg"), Res("rstd3")
            w3_ring = Ring(A, S, "w3", BF16, [NCH, 128], 17)
            sg_ring = Ring(A, S, "sg", F32, [2, SG], 4)
            hc3_ring = Ring(A, S, "hc3", F32, [PW3], 4)
            yc3_ring = Ring(A, S, "yc3", F32, [PW3], 3)
            pn3_ring = Ring(A, S, "pn3", F32, [PW3], 4)
            sq3_ring = Ring(A, S, "sq3", BF16, [PW3], 4)
            r_ys3 = [Res(f"ys3_{c}") for c in range(NCH)]
            gb3 = [0]

            def gbank3():
                b = gb3[0] % 6
                gb3[0] += 1
                return b

            def stats3(sq, rsq, c):
                for si, (c0, n) in enumerate(segs3):
                    S.op("pe", lambda e, si=si, c0=c0, n=n: e.matmul(bank[6 + si][:, 0:n], lhsT=onesb, rhs=sq[:, c0:c0 + n], start=(c == 0), stop=(c == NCH - 1)),
                         reads=[rsq, r_const], writes=[rbank[6 + si]])

            def load_inputs3(p):
                P0 = p * PW3
                for (dst, rdst, src) in ((u2t, r_u2t, u2s), (yrt, r_yrt, yrs), (yat, r_yat, yas)):
                    S.dma("sp", lambda e, dst=dst, src=src: e.dma_start(out=dst, in_=src[:, P0:P0 + PW3].rearrange("(a p) c -> p a c", p=128)), writes=[rdst])

            def merge_chunk(p, dc):
                sl = []
                for (W, c0w) in ((w_in, O_GRNN + dc * 128), (w_in, O_GATT + dc * 128), (w_ro, dc * 128), (w_ao, dc * 128)):
                    wt, rwt = w3_ring.next()
                    load_slab(wt, rwt, W, 0, NCH, c0w, 128)
                    sl.append((wt, rwt))
                srcs = ((u2t, r_u2t), (u2t, r_u2t), (yrt, r_yrt), (yat, r_yat))
                for si, (c0, n) in enumerate(segs3):
                    bs = [gbank3() for _ in range(4)]
                    for gi in range(4):
                        wt, rwt = sl[gi]
                        sa, rsa = srcs[gi]
                        b = bs[gi]
                        for k in range(NCH):
                            S.op("pe", lambda e, b=b, n=n, k=k, c0=c0, wt=wt, sa=sa: e.matmul(bank[b][:, 0:n], lhsT=wt[:, k, :], rhs=sa[:, k, c0:c0 + n], start=(k == 0), stop=(k == NCH - 1)),
                                 reads=[rwt, rsa], writes=[rbank[b]])
                    sg, rsg = sg_ring.next()
                    for gi in range(2):
                        S.op("act", lambda e, gi=gi, b=bs[gi], n=n, sg=sg: e.activation(out=sg[:, gi, 0:n], in_=bank[b][:, 0:n], func=AF.Sigmoid),
                             reads=[rbank[bs[gi]]], writes=[rsg])
                    for gi in range(2):
                        S.op("dve", lambda e, gi=gi, b=bs[2 + gi], n=n, sg=sg: e.tensor_tensor(out=sg[:, gi, 0:n], in0=bank[b][:, 0:n], in1=sg[:, gi, 0:n], op=ALU.mult),
                             reads=[rbank[bs[2 + gi]], rsg], writes=[rsg])
                    S.op("dve", lambda e, n=n, sg=sg, c0=c0: e.tensor_tensor(out=mg[:, dc, c0:c0 + n], in0=sg[:, 0, 0:n], in1=sg[:, 1, 0:n], op=ALU.add),
                         reads=[rsg], writes=[r_mg])

            def wo_chunk(p, dc, pending):
                P0 = p * PW3
                wt, rwt = w3_ring.next()
                load_slab(wt, rwt, w_o, 0, NCH, dc * 128, 128)
                bs = [gbank3(), gbank3()]
                for si, (c0, n) in enumerate(segs3):
                    b = bs[si]
                    for k in range(NCH):
                        S.op("pe", lambda e, b=b, n=n, k=k, c0=c0: e.matmul(bank[b][:, 0:n], lhsT=wt[:, k, :], rhs=mg[:, k, c0:c0 + n], start=(k == 0), stop=(k == NCH - 1)),
                             reads=[rwt, r_mg], writes=[rbank[b]])
                while pending:
                    pending.pop(0)()
                yc, ryc = yc3_ring.next()
                for si, (c0, n) in enumerate(segs3):
                    b = bs[si]
                    S.op("act", lambda e, b=b, n=n, c0=c0: e.activation(out=yc[:, c0:c0 + n], in_=bank[b][:, 0:n], func=AF.Copy),
                         reads=[rbank[b]], writes=[ryc])
                sq, rsq = sq3_ring.next()
                S.op("act", lambda e: e.activation(out=sq, in_=yc, func=AF.Square), reads=[ryc], writes=[rsq])
                pending.append(lambda: stats3(sq, rsq, dc))
                S.dma("sp", lambda e: e.dma_start(out=yscr[dc * 128:(dc + 1) * 128, P0:P0 + PW3], in_=yc), reads=[ryc], writes=[r_ys3[dc]])

            def postnorm3(p):
                P0 = p * PW3

                def rs():
                    for si, (c0, n) in enumerate(segs3):
                        rstd_from(bank[6 + si][:, 0:n], n, rstd3[:, c0:c0 + n], r_rstd3, rbank[6 + si])
                steps = [rs]

                ld = {}

                def loads(c):
                    if c >= NCH or c in ld:
                        return
                    pn, rpn = pn3_ring.next()
                    S.dma("sp", lambda e: e.dma_start(out=pn, in_=yscr[c * 128:(c + 1) * 128, P0:P0 + PW3]), reads=[r_ys3[c]], writes=[rpn])
                    hc, rhc = hc3_ring.next()
                    S.dma("sp", lambda e: e.dma_start(out=hc, in_=hT1[c * 128:(c + 1) * 128, P0:P0 + PW3]), writes=[rhc])
                    ld[c] = (pn, rpn, hc, rhc)

                def st(c):
                    loads(c)
                    loads(c + 1)
                    loads(c + 2)
                    pn, rpn, hc, rhc = ld.pop(c)
                    S.op("dve", lambda e: e.scalar_tensor_tensor(out=pn, in0=pn, scalar=gs_col(3, c), in1=rstd3, op0=ALU.mult, op1=ALU.mult),
                         reads=[rpn, r_rstd3, r_const], writes=[rpn])
                    S.op("dve", lambda e: e.tensor_tensor(out=pn, in0=pn, in1=hc, op=ALU.add), reads=[rpn, rhc], writes=[rpn])
                    S.dma("sp", lambda e: e.dma_start(out=hT2[c * 128:(c + 1) * 128, P0:P0 + PW3], in_=pn), reads=[rpn], writes=[])

                for c in range(NCH):
                    steps.append(lambda c=c: st(c))
                return steps

            load_inputs3(0)
            post = []
            for p in range(NP3):
                for dc in range(NCH):
                    merge_chunk(p, dc)
                    if post:
                        post.pop(0)()
                    if dc == 0 and post:
                        post.pop(0)()
                while post:
                    post.pop(0)()
                if p + 1 < NP3:
                    load_inputs3(p + 1)
                pending = []
                for dc in range(NCH):
                    wo_chunk(p, dc, pending)
                while pending:
                    pending.pop(0)()
                post = postnorm3(p)
            while post:
                post.pop(0)()
            S.barrier()
            A.release(m3)

        if "ffn1" in phases:
            ffn_phase(hT0, hT1, w_gu1, w_dn1, 0, 1)

        if "mix" in phases:
            mix_phase()

        fuse_out = "ffn2" in phases and "tout" in phases
        if "ffn2" in phases:
            ffn_phase(hT2 if "mix" in phases else hT1, hT3, w_gu2, w_dn2, 4, 5, final_out=fuse_out)

        if "tout" in phases and not fuse_out:
            m = A.mark()
            src = hT3 if "ffn2" in phases else (hT1 if "ffn1" in phases else hT0)
            st_ring = Ring(A, S, "st2", F32, [NCH, 128], 2)
            ot_ring = Ring(A, S, "ot", F32, [D], 2)
            for ti, (p0, n) in enumerate(pos_tiles()):
                st, rst = st_ring.next()
                ot, rot = ot_ring.next()
                S.dma("sp", lambda e, st=st, p0=p0, n=n: e.dma_start(out=st[:, :, 0:n], in_=src[:, p0:p0 + n].rearrange("(a p) c -> p a c", p=128)), writes=[rst])
                for g in range(4):
                    b = (ti * 4 + g) % 8
                    for j in range(4):
                        c = g * 4 + j
                        S.op("pe", lambda e, b=b, j=j, c=c, st=st, n=n: e.transpose(out=bank[b][0:n, j * 128:(j + 1) * 128], in_=st[:, c, 0:n], identity=ident),
                             reads=[rst, r_const], writes=[rbank[b]])
                    if g % 2 == 0:
                        S.op("act", lambda e, b=b, g=g, ot=ot, n=n: e.activation(out=ot[0:n, g * 512:(g + 1) * 512], in_=bank[b][0:n, :], func=AF.Copy),
                             reads=[rbank[b]], writes=[rot])
                    else:
                        S.op("dve", lambda e, b=b, g=g, ot=ot, n=n: e.tensor_copy(out=ot[0:n, g * 512:(g + 1) * 512], in_=bank[b][0:n, :]),
                             reads=[rbank[b]], writes=[rot])
                if ti == 0:
                    S.dma("sp", lambda e, ot=ot: e.dma_start(out=out[0:128 - NMETA, :], in_=ot[NMETA:128, :]), reads=[rot], writes=[])
                else:
                    S.dma("sp", lambda e, ot=ot, p0=p0, n=n: e.dma_start(out=out[p0 - NMETA:p0 - NMETA + n, :], in_=ot[0:n, :]), reads=[rot], writes=[])
            S.barrier()
            A.release(m)

        S.emit(block, sems, dsems)
    return nc


def host_consts(norm_g, conv_w, conv_b, lru_b_a, lru_b_x, lru_lambda, forget_b):
    cols = np.zeros((128, NCOLS), np.float32)

    def pc(v):
        return np.ascontiguousarray(np.asarray(v, np.float32).reshape(NCH, 128).T)

    for n in range(6):
        cols[:, C_G + n * 16:C_G + (n + 1) * 16] = pc(norm_g[0, n])
    for k in range(4):
        cols[:, C_CW + k * 16:C_CW + (k + 1) * 16] = pc(conv_w[0, k])
    cols[:, C_CB:C_CB + 16] = pc(conv_b[0])
    cols[:, C_BA:C_BA + 16] = pc(lru_b_a[0])
    cols[:, C_BX:C_BX + 16] = pc(lru_b_x[0])
    cols[:, C_LAM:C_LAM + 16] = pc(lru_lambda[0])
    cols[0:NH, C_FB] = np.asarray(forget_b[0], np.float32)
    ident = np.eye(128, dtype=np.float32)
    tri = np.triu(np.ones((128, 128), np.float32))
    return cols, ident, tri


_NC_CACHE = {}


def make_in_maps(inputs, batches):
    f = lambda a: np.ascontiguousarray(np.asarray(a, np.float32))
    cols, ident, tri = host_consts(inputs["norm_g"], inputs["conv_w"], inputs["conv_b"], inputs["lru_b_a"],
                                   inputs["lru_b_x"], inputs["lru_lambda"], inputs["forget_b"])
    shared = {
        "meta": f(inputs["meta_tokens"]), "cols": cols, "ident": ident, "tri": tri,
        "w_gu1": f(inputs["ffn1_w_gu"][0]), "w_dn1": f(inputs["ffn1_w_down"][0]), "w_in": f(inputs["w_in"][0]),
        "lru_wa": f(inputs["lru_w_a"][0]), "lru_wx": f(inputs["lru_w_x"][0]),
        "w_ro": f(inputs["w_rnn_out"][0]), "w_ao": f(inputs["w_attn_out"][0]), "w_o": f(inputs["w_o"][0]),
        "w_gu2": f(inputs["ffn2_w_gu"][0]), "w_dn2": f(inputs["ffn2_w_down"][0]),
    }
    xs = np.asarray(inputs["x"], np.float32)
    return [dict(shared, x=np.ascontiguousarray(xs[b])) for b in batches]


def kernel(**inputs):
    if "nc" not in _NC_CACHE:
        _NC_CACHE["nc"] = build()
    nc = _NC_CACHE["nc"]
    in_maps = make_in_maps(inputs, range(8))
    res = run_bass_kernel_spmd(nc, in_maps, core_ids=list(range(8)))
    return np.stack([np.asarray(r["out"], np.float32) for r in res.results], axis=0)
```
